# Optimizing a Trainium2 kernel written in Bass

```python
import math
import jax
import jax.numpy as jnp
from jax import lax
import numpy as np

D_MODEL = 1024
BATCH = 16
SEQ = 256
DEPTH = 2
DEC_BATCH = 8
DEC_SEQ = 1024
PAST_LEN = 512

GRID_W = 64
N_BRANCH = 4
BR_W = D_MODEL // 4
HD = 64
A_HEADS = BR_W // HD
A_QK = HD // 2
B_BLOCKS = 4
B_BLK = BR_W // B_BLOCKS
CONV_W = 4
CONV_LEFT = 1
LRU_C = 8.0
C_HEADS = BR_W // HD
NA_ROWS = 8
NA_COLS = 16
NA_QB = 16
NA_KB = NA_QB + NA_COLS
D_HEADS = BR_W // HD
D_KV = 2
D_GROUP = D_HEADS // D_KV
WIN = 128
QBLK = 128
ROPE_BASE = 10000.0
EPS = 1e-6
NEG = -1e30
IN_SIZES = (N_BRANCH * BR_W, 3 * A_HEADS * HD, BR_W, 3 * C_HEADS * HD, D_HEADS * HD, 2 * D_KV * HD)
IN_W = sum(IN_SIZES)
IN_SPLITS = tuple(int(s) for s in np.cumsum(IN_SIZES)[:-1])

kernel_name = 'hybrid_diffusion_prefix_step'


def rmsnorm(x, g):
    xf = x.astype(jnp.float32)
    y = xf * lax.rsqrt(jnp.mean(xf * xf, axis=-1, keepdims=True) + EPS)
    return (y * g.astype(jnp.float32)).astype(x.dtype)


def rope1d(x, pos):
    half = x.shape[-1] // 2
    inv = ROPE_BASE ** (-jnp.arange(half, dtype=jnp.float32) / half)
    ang = pos.astype(jnp.float32)[:, None] * inv[None, :]
    cos, sin = jnp.cos(ang), jnp.sin(ang)
    xf = x.astype(jnp.float32)
    x1, x2 = xf[..., :half], xf[..., half:]
    return jnp.concatenate([x1 * cos - x2 * sin, x1 * sin + x2 * cos], axis=-1).astype(x.dtype)


def rope2d(x):
    t = jnp.arange(x.shape[-2])
    half = x.shape[-1] // 2
    return jnp.concatenate([rope1d(x[..., :half], t // GRID_W), rope1d(x[..., half:], t % GRID_W)], axis=-1)


def heads(x, n):
    b, l, _ = x.shape
    return x.reshape(b, l, n, -1).transpose(0, 2, 1, 3)


def softmax_with_sink(s, sink):
    sk = jnp.broadcast_to(sink.astype(jnp.float32)[None, :, :, None, None], s.shape[:-1] + (1,))
    return jax.nn.softmax(jnp.concatenate([sk, s], axis=-1), axis=-1)[..., 1:]


def diff_attention(q, k, v, lam):
    b, h, _, lq, d = q.shape
    nb = lq // QBLK
    scale = d ** -0.5

    def block(qb):
        s = jnp.einsum('bhmqd,bhmkd->bhmqk', qb, k).astype(jnp.float32) * scale
        p = jax.nn.softmax(s, axis=-1)
        w = p[:, :, 0] - lam * p[:, :, 1]
        return jnp.einsum('bhqk,bhkd->bhqd', w.astype(v.dtype), v)

    qb = q.reshape(b, h, 2, nb, QBLK, d).transpose(3, 0, 1, 2, 4, 5)
    out = lax.map(block, qb)
    return out.transpose(1, 2, 0, 3, 4).reshape(b, h, lq, v.shape[-1])


def attend_dense(q, k, v, sink):
    b, kv, g, lq, d = q.shape
    nb = lq // QBLK
    scale = d ** -0.5

    def block(qb):
        s = jnp.einsum('bkgqd,bkcd->bkgqc', qb, k).astype(jnp.float32) * scale
        p = jax.nn.softmax(s, axis=-1) if sink is None else softmax_with_sink(s, sink)
        return jnp.einsum('bkgqc,bkcd->bkgqd', p.astype(v.dtype), v)

    qb = q.reshape(b, kv, g, nb, QBLK, d).transpose(3, 0, 1, 2, 4, 5)
    out = lax.map(block, qb)
    return out.transpose(1, 2, 3, 0, 4, 5).reshape(b, kv, g, lq, v.shape[-1])


def swa_latent(q, k, v, k_ctx, v_ctx, sink):
    b, kv, g, L, d = q.shape
    nb = L // QBLK
    span = QBLK + 2 * WIN
    scale = d ** -0.5
    n_ctx = k_ctx.shape[2]
    pad = ((0, 0), (0, 0), (WIN, WIN), (0, 0))
    kp, vp = jnp.pad(k, pad), jnp.pad(v, pad)

    def block(xs):
        qb, j = xs
        start = j * QBLK
        kb = lax.dynamic_slice_in_dim(kp, start, span, axis=2)
        vb = lax.dynamic_slice_in_dim(vp, start, span, axis=2)
        qi = start + jnp.arange(QBLK)
        ki = start - WIN + jnp.arange(span)
        ok = (jnp.abs(qi[:, None] - ki[None, :]) <= WIN) & (ki >= 0)[None, :] & (ki < L)[None, :]
        s_loc = jnp.where(ok, jnp.einsum('bkgqd,bkjd->bkgqj', qb, kb).astype(jnp.float32) * scale, NEG)
        s_ctx = jnp.einsum('bkgqd,bkcd->bkgqc', qb, k_ctx).astype(jnp.float32) * scale
        p = softmax_with_sink(jnp.concatenate([s_ctx, s_loc], axis=-1), sink).astype(v.dtype)
        return (jnp.einsum('bkgqc,bkcd->bkgqd', p[..., :n_ctx], v_ctx)
                + jnp.einsum('bkgqj,bkjd->bkgqd', p[..., n_ctx:], vb))

    qb = q.reshape(b, kv, g, nb, QBLK, d).transpose(3, 0, 1, 2, 4, 5)
    out = lax.map(block, (qb, jnp.arange(nb, dtype=jnp.int32)))
    return out.transpose(1, 2, 3, 0, 4, 5).reshape(b, kv, g, L, d)


def na_latent(q, k, v, k_ctx, v_ctx, rpb):
    b, h, L, d = q.shape
    rows = L // GRID_W
    wr = min(NA_ROWS, rows)
    ncb = GRID_W // NA_QB
    scale = d ** -0.5
    n_ctx = k_ctx.shape[2]
    row_start = np.clip(np.arange(rows) - wr // 2, 0, rows - wr)
    qcols = np.arange(GRID_W).reshape(ncb, NA_QB)
    col_start = np.clip(qcols - NA_COLS // 2, 0, GRID_W - NA_COLS)
    band = (np.clip(np.arange(ncb) * NA_QB - NA_COLS // 2, 0, GRID_W - NA_KB)[:, None]
            + np.arange(NA_KB)[None, :])
    col_ok = (band[:, None, :] >= col_start[:, :, None]) & (band[:, None, :] < col_start[:, :, None] + NA_COLS)
    dcol = np.clip(band[:, None, :] - qcols[:, :, None] + NA_COLS - 1, 0, 2 * NA_COLS - 2)
    kg = k.reshape(b, h, rows, GRID_W, d)
    vg = v.reshape(b, h, rows, GRID_W, d)
    rpbf = rpb.astype(jnp.float32)

    def row_block(xs):
        qr, rs, ri = xs
        kb = lax.dynamic_slice_in_dim(kg, rs, wr, axis=2)[:, :, :, band]
        vb = lax.dynamic_slice_in_dim(vg, rs, wr, axis=2)[:, :, :, band]
        s_loc = jnp.einsum('bhnqd,bhrnkd->bhnqrk', qr, kb).astype(jnp.float32) * scale
        drow = rs + jnp.arange(wr) - ri + NA_ROWS - 1
        bias = rpbf[:, drow[:, None, None, None], dcol[None]].transpose(0, 2, 3, 1, 4)
        s_loc = jnp.where(col_ok[:, :, None, :], s_loc + bias, NEG).reshape(b, h, ncb, NA_QB, wr * NA_KB)
        s_ctx = jnp.einsum('bhnqd,bhcd->bhnqc', qr, k_ctx).astype(jnp.float32) * scale
        p = jax.nn.softmax(jnp.concatenate([s_ctx, s_loc], axis=-1), axis=-1).astype(v.dtype)
        p_loc = p[..., n_ctx:].reshape(b, h, ncb, NA_QB, wr, NA_KB)
        return (jnp.einsum('bhnqc,bhcd->bhnqd', p[..., :n_ctx], v_ctx)
                + jnp.einsum('bhnqrk,bhrnkd->bhnqd', p_loc, vb))

    qr = q.reshape(b, h, rows, ncb, NA_QB, d).transpose(2, 0, 1, 3, 4, 5)
    out = lax.map(row_block, (qr, jnp.asarray(row_start, jnp.int32), jnp.arange(rows, dtype=jnp.int32)))
    return out.transpose(1, 2, 0, 3, 4, 5).reshape(b, h, L, d)


def conv_centred(x, w, bias):
    L = x.shape[1]
    xp = jnp.pad(x, ((0, 0), (CONV_LEFT, CONV_W - 1 - CONV_LEFT), (0, 0)))
    y = bias
    for j in range(CONV_W):
        y = y + xp[:, j:j + L] * w[j]
    return y


def rglru(x, wa, ba, wx, bx, lam, h0, reverse):
    f32 = jnp.float32
    b, L, W = x.shape
    xf = x.astype(f32)
    xb = xf.reshape(b, L, B_BLOCKS, B_BLK)
    r = jax.nn.sigmoid(jnp.einsum('blnj,njk->blnk', xb, wa.astype(f32)).reshape(b, L, W) + ba.astype(f32))
    i = jax.nn.sigmoid(jnp.einsum('blnj,njk->blnk', xb, wx.astype(f32)).reshape(b, L, W) + bx.astype(f32))
    log_a = -LRU_C * r * jax.nn.softplus(-lam.astype(f32))
    a = jnp.exp(log_a)
    u = jnp.sqrt(-jnp.expm1(2.0 * log_a)) * (i * xf)

    def step(hc, au):
        hc = au[0] * hc + au[1]
        return hc, hc

    h_last, hs = lax.scan(step, h0.astype(f32), (a.swapaxes(0, 1), u.swapaxes(0, 1)), reverse=reverse)
    return hs.swapaxes(0, 1), h_last


def merge_out(h, br, gates, w_mg, b_mg, w_bo, w_o):
    g = jax.nn.sigmoid(jnp.einsum('bld,dnm->blnm', h, w_mg) + b_mg)
    proj = jnp.einsum('blnw,nwm->blnm', br * jax.nn.silu(gates), w_bo)
    return jnp.einsum('blnm,blnm->blm', g, proj) @ w_o


def layer(x, cvec, lp, l, cache):
    (norm_g, w_ada, b_ada, w_in, diff_lam, diff_g, conv_w, conv_b, lru_wa, lru_ba, lru_wx, lru_bx,
     lru_lam, na_rpb, swa_sink, w_mg, b_mg, w_bo, w_o) = lp
    f32 = jnp.float32
    b, L, _ = x.shape
    mod = (jax.nn.silu(cvec) @ w_ada + b_ada)[:, None, :]
    shift, scale, gate = jnp.split(mod, 3, axis=-1)
    h = rmsnorm(x, norm_g) * (1.0 + scale) + shift
    g_br, a_qkv, b_x, c_qkv, d_q, d_kv = jnp.split(h @ w_in, IN_SPLITS, axis=-1)

    aq, ak, av = jnp.split(a_qkv, 3, axis=-1)
    aq = aq.reshape(b, L, A_HEADS, 2, A_QK).transpose(0, 2, 3, 1, 4)
    ak = ak.reshape(b, L, A_HEADS, 2, A_QK).transpose(0, 2, 3, 1, 4)
    av = heads(av, A_HEADS)
    lam_init = 0.8 - 0.6 * math.exp(-0.3 * l)
    lf = diff_lam.astype(f32)
    lam = jnp.exp(jnp.sum(lf[0] * lf[1])) - jnp.exp(jnp.sum(lf[2] * lf[3])) + lam_init

    cq, ck, cv = [heads(t, C_HEADS) for t in jnp.split(c_qkv, 3, axis=-1)]
    dq = d_q.reshape(b, L, D_KV, D_GROUP, HD).transpose(0, 2, 3, 1, 4)
    dk, dv = [heads(t, D_KV) for t in jnp.split(d_kv, 2, axis=-1)]
    sink = swa_sink.reshape(D_KV, D_GROUP)
    xc = conv_centred(b_x, conv_w, conv_b)

    if cache is None:
        ya = diff_attention(aq, ak, av, lam)
        h0f = jnp.zeros((b, BR_W), f32)
        h0b = jnp.zeros((b, BR_W), f32)
        yc = attend_dense(cq[:, :, None], ck, cv, None)[:, :, 0]
        yd = attend_dense(dq, dk, dv, sink)
    else:
        ck_a, cv_a, ck_c, cv_c, ck_d, cv_d, st = cache
        ya = diff_attention(rope2d(aq), jnp.concatenate([ck_a, rope2d(ak)], axis=3),
                            jnp.concatenate([cv_a, av], axis=2), lam)
        h0f, h0b = st[:, 0], st[:, 1]
        yc = na_latent(cq, ck, cv, ck_c, cv_c, na_rpb)
        yd = swa_latent(rope2d(dq), rope2d(dk), dv, ck_d, cv_d, sink)

    ya = (rmsnorm(ya, diff_g) * (1.0 - lam_init)).transpose(0, 2, 1, 3).reshape(b, L, BR_W)
    yf, hf = rglru(xc, lru_wa[0], lru_ba[0], lru_wx[0], lru_bx[0], lru_lam[0], h0f, False)
    yb, hb = rglru(xc, lru_wa[1], lru_ba[1], lru_wx[1], lru_bx[1], lru_lam[1], h0b, True)
    yr = (yf + yb).astype(x.dtype)
    yc = yc.transpose(0, 2, 1, 3).reshape(b, L, BR_W)
    yd = yd.transpose(0, 3, 1, 2, 4).reshape(b, L, BR_W)
    br = jnp.stack([ya, yr, yc, yd], axis=2)
    out = merge_out(h, br, g_br.reshape(b, L, N_BRANCH, BR_W), w_mg, b_mg, w_bo, w_o)
    x = x + gate * out
    if cache is None:
        return x, (ak, av, ck, cv, dk, dv, jnp.stack([hf, hb], axis=1).astype(x.dtype))
    return x, None


def setup_inputs(seed: int = 0) -> dict:
    key = jax.random.key(seed)
    ks = list(jax.random.split(key, 32))
    f32 = jnp.float32

    def nrm(i, shape, s):
        return jax.random.normal(ks[i], shape, f32) * s

    a0 = jax.random.uniform(ks[31], (DEPTH, 2, BR_W), f32, 0.9, 0.999)
    return {
        'x_prompt': nrm(0, (BATCH, SEQ, D_MODEL), 1.0),
        'x_sample': nrm(1, (DEC_BATCH, DEC_SEQ, D_MODEL), 1.0),
        'cache_diff_k': nrm(2, (DEC_BATCH, DEPTH, A_HEADS, 2, PAST_LEN, A_QK), 1.0),
        'cache_diff_v': nrm(3, (DEC_BATCH, DEPTH, A_HEADS, PAST_LEN, HD), 1.0),
        'cache_na_k': nrm(4, (DEC_BATCH, DEPTH, C_HEADS, PAST_LEN, HD), 1.0),
        'cache_na_v': nrm(5, (DEC_BATCH, DEPTH, C_HEADS, PAST_LEN, HD), 1.0),
        'cache_swa_k': nrm(6, (DEC_BATCH, DEPTH, D_KV, PAST_LEN, HD), 1.0),
        'cache_swa_v': nrm(7, (DEC_BATCH, DEPTH, D_KV, PAST_LEN, HD), 1.0),
        'state_lru': nrm(8, (DEC_BATCH, DEPTH, 2, BR_W), 0.5),
        'c': nrm(9, (DEC_BATCH, D_MODEL), 1.0),
        'c_ctx': nrm(10, (D_MODEL,), 1.0),
        'norm_g': 1.0 + nrm(11, (DEPTH, D_MODEL), 0.02),
        'w_ada': nrm(12, (DEPTH, D_MODEL, 3 * D_MODEL), 0.5 * D_MODEL ** -0.5),
        'b_ada': nrm(13, (DEPTH, 3 * D_MODEL), 0.02),
        'w_in': nrm(14, (DEPTH, D_MODEL, IN_W), D_MODEL ** -0.5),
        'diff_lambda': nrm(15, (DEPTH, 4, A_QK), 0.1),
        'diff_norm_g': 1.0 + nrm(16, (DEPTH, HD), 0.02),
        'conv_w': nrm(17, (DEPTH, CONV_W, BR_W), CONV_W ** -0.5),
        'conv_b': nrm(18, (DEPTH, BR_W), 0.02),
        'lru_wa': nrm(19, (DEPTH, 2, B_BLOCKS, B_BLK, B_BLK), B_BLK ** -0.5),
        'lru_ba': nrm(20, (DEPTH, 2, BR_W), 0.02),
        'lru_wx': nrm(21, (DEPTH, 2, B_BLOCKS, B_BLK, B_BLK), B_BLK ** -0.5),
        'lru_bx': nrm(22, (DEPTH, 2, BR_W), 0.02),
        'lru_lam': jnp.log(a0) - jnp.log1p(-a0),
        'na_rpb': nrm(23, (DEPTH, C_HEADS, 2 * NA_ROWS - 1, 2 * NA_COLS - 1), 0.1),
        'swa_sink': nrm(24, (DEPTH, D_HEADS), 0.5),
        'w_mg': nrm(25, (DEPTH, D_MODEL, N_BRANCH, D_MODEL), D_MODEL ** -0.5),
        'b_mg': nrm(26, (DEPTH, N_BRANCH, D_MODEL), 0.02),
        'w_bo': nrm(27, (DEPTH, N_BRANCH, BR_W, D_MODEL), BR_W ** -0.5),
        'w_o': nrm(28, (DEPTH, D_MODEL, D_MODEL), D_MODEL ** -0.5),
        'norm_f': 1.0 + nrm(29, (D_MODEL,), 0.02),
    }


def reference(x_prompt, x_sample, cache_diff_k, cache_diff_v, cache_na_k, cache_na_v, cache_swa_k, cache_swa_v,
              state_lru, c, c_ctx, norm_g, w_ada, b_ada, w_in, diff_lambda, diff_norm_g, conv_w, conv_b,
              lru_wa, lru_ba, lru_wx, lru_bx, lru_lam, na_rpb, swa_sink, w_mg, b_mg, w_bo, w_o, norm_f):
    xp = x_prompt
    xs = x_sample
    per_layer = []
    for l in range(DEPTH):
        lp = (norm_g[l], w_ada[l], b_ada[l], w_in[l], diff_lambda[l], diff_norm_g[l], conv_w[l], conv_b[l],
              lru_wa[l], lru_ba[l], lru_wx[l], lru_bx[l], lru_lam[l], na_rpb[l], swa_sink[l],
              w_mg[l], b_mg[l], w_bo[l], w_o[l])
        xp, st = layer(xp, c_ctx[None, :], lp, l, None)
        per_layer.append(st)
        cache_l = (cache_diff_k[:, l], cache_diff_v[:, l], cache_na_k[:, l], cache_na_v[:, l],
                   cache_swa_k[:, l], cache_swa_v[:, l], state_lru[:, l])
        xs, _ = layer(xs, c, lp, l, cache_l)
    y_prompt = rmsnorm(xp, norm_f)
    y_sample = rmsnorm(xs, norm_f)
    stacked = [jnp.stack([per_layer[l][i] for l in range(DEPTH)], axis=1) for i in range(7)]
    new_diff_k, new_diff_v, new_na_k, new_na_v, new_swa_k, new_swa_v, new_state_lru = stacked
    return (y_prompt, y_sample, new_diff_k, new_diff_v, new_na_k, new_na_v, new_swa_k, new_swa_v, new_state_lru)
```

```python
import math
import numpy as np
from contextlib import ExitStack
import concourse.bass as bass
import concourse.mybir as mybir
from concourse.bass_utils import run_bass_kernel_spmd

F32 = mybir.dt.float32
BF16 = mybir.dt.bfloat16
ALU = mybir.AluOpType
AF = mybir.ActivationFunctionType
AX = mybir.AxisListType

NCORES = 8
D = 1024
DEPTH = 2
LT = 1024
PT = 512
NT = LT + PT
PAST = 512
EPS = 1e-6
MASKV = -30000.0
NS = 256
ENGS = ("pe", "act", "dve", "pool", "sp")

C_ID, C_I8, C_ONES, C_BONES, C_RA, C_RD = 0, 128, 256, 384, 512, 640
C_COSA, C_SINA, C_COSD, C_SIND = 768, 1792, 2816, 3840
C_BAND, C_MA, C_MB = 4864, 6016, 7040
NCB = 8064
NCF = 132
NSTRIP = 22 * 64
STOP = None


class Sched:
    def __init__(self):
        self.ops = []
        self.last_writer = {}
        self.readers = {}
        self.dma_keys = []

    def op(self, eng, fn, reads=(), writes=(), dma=None):
        i = len(self.ops)
        deps = set()
        px = [r for r in reads if isinstance(r, tuple) and r[0] in ("PS", "PO")]
        if px:
            reads = [r for r in reads if r not in px]
            writes = list(writes) + [r for r in px if r not in writes]
        for r in reads:
            w = self.last_writer.get(r)
            if w is not None:
                deps.add(w)
        for r in writes:
            w = self.last_writer.get(r)
            if w is not None:
                deps.add(w)
            for x in self.readers.get(r, ()):
                deps.add(x)
        for r in dict.fromkeys(reads):
            lst = self.readers.setdefault(r, [])
            if dma is None:
                lst[:] = [x for x in lst if not (self.ops[x]["dma"] is None and self.ops[x]["eng"] == eng)]
            lst.append(i)
        for r in writes:
            self.last_writer[r] = i
            self.readers[r] = []
        if dma is not None and dma not in self.dma_keys:
            self.dma_keys.append(dma)
        self.ops.append(dict(eng=eng, fn=fn, deps=deps, dma=dma))
        return i

    def emit(self, nc, stack, final_wait_eng="sp"):
        ops = self.ops
        n = len(ops)

        def skip(o, pj):
            return pj["dma"] is None and pj["eng"] == "pe" and o["eng"] == "pe" and o["dma"] is None

        signaling = [False] * n
        for o in ops:
            for j in o["deps"]:
                pj = ops[j]
                if pj["dma"] is None and not skip(o, pj):
                    signaling[j] = True
        esem = {e: stack.enter_context(nc.semaphore("s_" + e)) for e in ("pe", "act", "dve", "pool")}
        dsem = {k: stack.enter_context(nc.semaphore("d_%d" % idx)) for idx, k in enumerate(self.dma_keys)}
        cnt = {e: 0 for e in esem}
        dcnt = {k: 0 for k in dsem}
        token = [None] * n
        for i, o in enumerate(ops):
            if o["dma"] is not None:
                dcnt[o["dma"]] += 16
                token[i] = (("d", o["dma"]), dcnt[o["dma"]])
            elif signaling[i]:
                cnt[o["eng"]] += 1
                token[i] = (("e", o["eng"]), cnt[o["eng"]])
        per_eng = {e: [] for e in ENGS}
        for i, o in enumerate(ops):
            per_eng[o["eng"]].append(i)
        known = {e: {} for e in ENGS}
        known_at = [None] * n
        plan = [None] * n
        for i, o in enumerate(ops):
            E = o["eng"]
            kn = known[E]
            best = {}
            for j in sorted(o["deps"], reverse=True):
                pj = ops[j]
                if skip(o, pj):
                    continue
                sk, val = token[j]
                if kn.get(sk, 0) >= val:
                    continue
                best[sk] = max(best.get(sk, 0), val)
                kn[sk] = val
                ka = known_at[j]
                if ka:
                    for k2, v2 in ka.items():
                        if kn.get(k2, 0) < v2:
                            kn[k2] = v2
            plan[i] = [(sk, v) for sk, v in best.items()]
            ka = dict(kn)
            if token[i] is not None and o["dma"] is None:
                ka[token[i][0]] = token[i][1]
            known_at[i] = ka
        self.stats = dict(n_ops=n, n_signal=sum(signaling), n_waits=sum(len(p) for p in plan),
                          per_eng={e: len(v) for e, v in per_eng.items()}, n_dma_sems=len(dsem))

        def semof(sk):
            return esem[sk[1]] if sk[0] == "e" else dsem[sk[1]]

        block = stack.enter_context(nc.Block())
        engobj = {"pe": "tensor", "act": "scalar", "dve": "vector", "pool": "gpsimd", "sp": "sync"}

        def make(E):
            def body(eng):
                for i in per_eng[E]:
                    o = ops[i]
                    for sk, val in plan[i]:
                        eng.wait_ge(semof(sk), val)
                    inst = o["fn"](eng)
                    if token[i] is not None:
                        sk, val = token[i]
                        inst.then_inc(semof(sk), 16 if sk[0] == "d" else 1)
                if E == final_wait_eng:
                    for k in self.dma_keys:
                        if dcnt[k] > 0:
                            eng.wait_ge(dsem[k], dcnt[k])
            return body

        for E in ENGS:
            if per_eng[E] or E == final_wait_eng:
                getattr(block, engobj[E])(make(E))


class Rot:
    def __init__(self, name, tiles, keys=None):
        self.name, self.tiles, self.i = name, tiles, 0
        self.keys = keys if keys is not None else [(name, k) for k in range(len(tiles))]

    def get(self):
        k = self.i % len(self.tiles)
        self.i += 1
        return self.tiles[k], self.keys[k]


def build_program():
    nc = bass.Bass("TRN2", target_bir_lowering=False)
    st = ExitStack()
    S = Sched()

    def din(name, shape):
        return nc.dram_tensor(name, list(shape), F32, kind="ExternalInput").ap()

    def dout(name, shape):
        return nc.dram_tensor(name, list(shape), F32, kind="ExternalOutput").ap()

    x_all = din("x_all", [NT, D])
    w_ada = din("w_ada", [DEPTH, D, 3 * D])
    w_in = din("w_in", [DEPTH, D, 3328])
    w_mg = din("w_mg", [DEPTH, D, 4, D])
    w_bo = din("w_bo", [DEPTH, 4, 256, D])
    w_o = din("w_o", [DEPTH, D, D])
    small_d = din("small", [128, DEPTH, NS])
    cT_d = din("cT", [128, 8, 2])
    normf_d = din("normf", [128, 8])
    lruw_d = din("lruw", [DEPTH, 128, 8, 128])
    strips_d = din("strips", [DEPTH, 4, 128, NSTRIP])
    cstb_d = din("cstb", [128, NCB])
    cstf_d = din("cstf", [128, NCF])
    cdk_d = din("cdk", [DEPTH, 4, 2, PAST, 32])
    cdv_d = din("cdv", [DEPTH, 4, PAST, 64])
    cnk_d = din("cnk", [DEPTH, 4, PAST, 64])
    cnv_d = din("cnv", [DEPTH, 4, PAST, 64])
    csk_d = din("csk", [DEPTH, 2, PAST, 64])
    csv_d = din("csv", [DEPTH, 2, PAST, 64])

    y_all = dout("y_all", [NT, D])
    ndk = dout("ndk", [2, DEPTH, 4, 2, 256, 32])
    ndv = dout("ndv", [2, DEPTH, 4, 256, 64])
    nnk = dout("nnk", [2, DEPTH, 4, 256, 64])
    nnv = dout("nnv", [2, DEPTH, 4, 256, 64])
    nsk = dout("nsk", [2, DEPTH, 2, 256, 64])
    nsv = dout("nsv", [2, DEPTH, 2, 256, 64])
    nst = dout("nst", [2, DEPTH, 2, 256])

    wsc = nc.dram_tensor("wsc", [DEPTH, 17, 128, 8, 512], BF16, kind="Internal").ap()
    wsc_bo = nc.dram_tensor("wsc_bo", [DEPTH, 8, 128, 8, 128], BF16, kind="Internal").ap()

    def sb(name, shape, dt=F32):
        return st.enter_context(nc.sbuf_tensor(name, list(shape), dt))

    xT = sb("xT", [128, 8, NT])
    hT = sb("hT", [128, 8192], BF16)
    brg = sb("brg", [128, 8, 1024], BF16)
    qk = sb("qk", [128, 14, 1024], BF16)
    VA = sb("VA", [128, 8, 384], BF16)
    VC = sb("VC", [128, 8, 384], BF16)
    VD = sb("VD", [128, 8, 320], BF16)
    bxp = sb("bxp", [128, 2, 1040], BF16)
    xc = sb("xc", [128, 2, 1024], BF16)
    yrf = bxp
    wbufs = [sb("wb%d" % i, [128, 8, 512], BF16) for i in range(2)]
    wbos = [sb("wbo%d" % i, [128, 8, 128], BF16) for i in range(2)]
    cstb = sb("cstb_s", [128, NCB], BF16)
    cstf = sb("cstf_s", [128, NCF])
    stripb = [sb("strip%d" % i, [128, NSTRIP], BF16) for i in range(2)]
    lruw = sb("lruw_s", [128, 8, 128], BF16)
    cdiag = sb("cdiag", [128, 8, 128], BF16)
    small = sb("small_s", [128, DEPTH, NS])
    cT = sb("cT_s", [128, 8, 2])
    scT = sb("scT", [128, 8, 2], BF16)
    normf = sb("normf_s", [128, 8])
    modTs = [sb("modT%d" % i, [128, 24, 2]) for i in range(DEPTH)]
    sc1s = [sb("sc1_%d" % i, [128, 8, 2]) for i in range(DEPTH)]
    der = sb("der", [128, 32])
    hlast = sb("hlast", [128, 8])
    stout = sb("stout", [128, 8])
    Fp = Rot("F", [sb("F%d" % i, [128, 512]) for i in range(6)])
    Rp = Rot("R", [sb("R0", [128, 512])])
    Bp = Rot("B", [sb("B%d" % i, [128, 512], BF16) for i in range(2)])
    Bpt = Rot("BT", [sb("BT%d" % i, [128, 512], BF16) for i in range(8)])
    dummy = sb("dmy_t", [128, 4])
    _pst = [st.enter_context(nc.psum_tensor("ps%d" % i, [128, 512], F32)) for i in range(8)]
    PSp = Rot("PS", _pst)
    PSa = Rot("PS", _pst[0:4], keys=[("PS", k) for k in range(0, 4)])
    PSo = Rot("PS", _pst[4:8], keys=[("PS", k) for k in range(4, 8)])
    CK = [("kstg", 0), ("kstg", 1), ("kstg", 2), "cKA0", "cKA1", "cKC0", "cKC1", "cKD0", "cKD1"] + [("cVA", h) for h in range(5)] + [("cVC", h) for h in range(5)] + [("cVD", h) for h in range(3)]

    def fence(tag):
        S.op("pool", lambda e: e.memset(dummy[:, 0:1], 0.0), reads=[], writes=HT_ALL + CK + ["cfence"])

    HT_ALL = [("hT", i) for i in range(16)]

    def hkeys(kc, lo, hi):
        return [("hT", kc * 2 + b) for b in range(lo // 512, (hi - 1) // 512 + 1)]

    def hview(off, shape):
        n = int(np.prod(shape))
        v = hT[:, off:off + n]
        if len(shape) == 2:
            return v.rearrange("p (a b) -> p a b", a=shape[0])
        return v

    cKA = hview(0, [2, 512])
    cKC = hview(1024, [2, 512])
    cKD = hview(2048, [2, 512])
    cVA = hview(3072, [4, 384])
    cVC = hview(3072 + 1536, [4, 384])
    cVD = hview(3072 + 3072, [4, 320])

    def dma(eng, out, in_, reads, writes, key):
        S.op(eng, lambda e: e.dma_start(out=out, in_=in_), reads=reads, writes=writes, dma=key)

    def mm(out, lhsT, rhs, start, stop, reads, writes):
        if "cstb" in reads:
            reads = list(reads) + ["cstb2"]
        S.op("pe", lambda e: e.matmul(out, lhsT=lhsT, rhs=rhs, start=start, stop=stop), reads=reads, writes=writes)

    def act(out, in_, func, reads, writes, bias=None, scale=None):
        kw = {}
        if bias is not None:
            kw["bias"] = bias
        if scale is not None:
            kw["scale"] = scale
        S.op("act", lambda e: e.activation(out=out, in_=in_, func=func, **kw), reads=reads, writes=writes)

    def tt(out, in0, in1, op, reads, writes, eng="dve"):
        if "cstb" in reads:
            reads = list(reads) + ["cstb2"]
        S.op(eng, lambda e: e.tensor_tensor(out=out, in0=in0, in1=in1, op=op), reads=reads, writes=writes)

    def ts(out, in0, s1, op0, reads, writes, s2=None, op1=None, eng="dve"):
        if op1 is None:
            S.op(eng, lambda e: e.tensor_scalar(out=out, in0=in0, scalar1=s1, scalar2=None, op0=op0), reads=reads, writes=writes)
        else:
            S.op(eng, lambda e: e.tensor_scalar(out=out, in0=in0, scalar1=s1, scalar2=s2, op0=op0, op1=op1), reads=reads, writes=writes)

    def stt(out, in0, scalar, in1, op0, op1, reads, writes):
        S.op("dve", lambda e: e.scalar_tensor_tensor(out=out, in0=in0, scalar=scalar, in1=in1, op0=op0, op1=op1),
             reads=reads, writes=writes)

    def cp(out, in_, reads, writes, eng="dve"):
        S.op(eng, lambda e: e.tensor_copy(out=out, in_=in_), reads=reads, writes=writes)

    def recip(out, in_, reads, writes):
        S.op("dve", lambda e: e.reciprocal(out=out, in_=in_), reads=reads, writes=writes)

    def memset(ap, val, writes, eng="dve"):
        S.op(eng, lambda e: e.memset(ap, val), writes=writes)

    ident_b = cstb[:, C_ID:C_ID + 128]
    i8_b = cstb[:, C_I8:C_I8 + 128]
    ones_b = cstb[:, C_ONES:C_ONES + 128]
    bones_b = cstb[:, C_BONES:C_BONES + 128]
    ident_f = cstf[:, 0:128]
    eps_c = cstf[:, 130:131]

    dma("pool", cstb[:, 0:768], cstb_d[:, 0:768], [], ["cstb"], "su0")
    dma("sp", cstf[:], cstf_d, [], ["cstf"], "su1")
    dma("sp", small[:], small_d, [], ["small"], "su2")
    dma("sp", cT[:], cT_d, [], ["cT"], "su3")
    dma("sp", normf[:], normf_d, [], ["normf"], "su4")
    for (Vt, name, nt, cols) in ((VA, "VA", 8, (64, 256)), (VC, "VC", 8, (64, 256)), (VD, "VD", 8, (0, 128, 256))):
        for c0 in cols:
            memset(Vt[:, :, c0:c0 + 64], 1.0, [(name, t) for t in range(nt)], eng="pool")
    memset(bxp[:], 0.0, ["bxp"], eng="pool")
    act(scT[:], cT[:], AF.Silu, ["cT"], ["scT"])

    wb_rot = Rot("wb", wbufs)
    wbo_rot = Rot("wbo", wbos)

    def load_w(src_ap, ncols=512, four=False, cache=None):
        wt, wk = wb_rot.get()
        allk = [wk] + [(wk, n) for n in range(4)]
        if cache is not None and not cache[2]:
            l_, idx, _ = cache
            dma("pool", wt[:, :, 0:ncols], wsc[l_, idx, :, :, 0:ncols], [("wsc", l_, idx)], allk, wk)
            return wt, allk
        if four:
            for n in range(4):
                dma("pool", wt[:, :, n * 128:(n + 1) * 128], src_ap(n), [], [(wk, n)] + ([wk] if n == 0 else []), wk)
        elif ncols == 512:
            dma("pool", wt[:], src_ap, [], allk, wk)
        else:
            dma("pool", wt[:, :, 0:ncols], src_ap, [], allk, wk)
        if cache is not None and cache[2]:
            l_, idx, _ = cache
            dma("sp", wsc[l_, idx, :, :, 0:ncols], wt[:, :, 0:ncols], allk, [("wsc", l_, idx)], ("wst", wk[1]))
        return wt, allk

    def rms_rstd(tok0, blk3, lnexp=False):
        ps, pk = PSp.get()
        for kc in range(8):
            sq, sqk = Bp.get()
            act(sq[:], xT[:, kc, tok0:tok0 + 512], AF.Square, [("xT", kc, blk3)], [sqk])
            mm(ps[:], ones_b, sq[:], kc == 0, kc == 7, [sqk, "cstb"], [pk])
        r, rk = Rp.get()
        if lnexp:
            act(r[:], ps[:], AF.Ln, [pk, "cstf"], [rk], bias=eps_c, scale=1.0 / D)
            act(r[:], r[:], AF.Exp, [rk], [rk], scale=-0.5)
        else:
            act(r[:], ps[:], AF.Sqrt, [pk, "cstf"], [rk], bias=eps_c, scale=1.0 / D)
            recip(r[:], r[:], [rk], [rk])
        return r, rk

    def emit_h(P, b, l, lnexp=False):
        modT, sc1 = modTs[l], sc1s[l]
        tok0 = P["t0"] + 512 * b
        blk3 = tok0 // 512
        g = P["g"]
        r, rk = rms_rstd(tok0, blk3, lnexp)
        for kc in range(8):
            tmp, tk = Fp.get()
            tt(tmp[:], xT[:, kc, tok0:tok0 + 512], r[:], ALU.mult, [("xT", kc, blk3), rk], [tk])
            act(hT[:, kc * 1024 + 512 * b: kc * 1024 + 512 * b + 512], tmp[:], AF.Identity, [tk, ("sc1", l), ("modT", l)], [("hT", kc * 2 + b)],
                bias=modT[:, kc, g:g + 1], scale=sc1[:, kc, g:g + 1])

    def hsl(kc, lo, n):
        return hT[:, kc * 1024 + lo: kc * 1024 + lo + n]

    def fm_proj(wt, wk, col0, P, b):
        ps, pk = PSp.get()
        for kc in range(8):
            mm(ps[:], wt[:, kc, col0:col0 + 128], hsl(kc, 512 * b, 512), kc == 0, kc == 7, wk + [("hT", kc * 2 + b)], [pk])
        return ps, pk

    def tm_proj(wt, wk, col0, ncols, tile):
        ps, pk = PSp.get()
        for kc in range(8):
            mm(ps[:, 0:ncols], hsl(kc, tile * 128, 128), wt[:, kc, col0:col0 + ncols], kc == 0, kc == 7,
               wk + [("hT", kc * 2 + tile // 4)], [pk])
        return ps, pk

    def mod_gen(l, pool, blocks=(0, 1, 2, 3, 4, 5)):
        sm = small[:, l, :]
        modT, sc1 = modTs[l], sc1s[l]
        pend = None
        for step in range(len(blocks) + 1):
            cur = None
            j = blocks[step] if step < len(blocks) else None
            if j is not None:
                cur = load_w(w_ada[l][:, j * 512:(j + 1) * 512].rearrange("(kc p) n -> p kc n", p=128))
            if pend is not None:
                (wt, wk), jj = pend
                ps, pk = pool.get()
                for o4 in range(4):
                    for kc in range(8):
                        mm(ps[:, o4 * 2:o4 * 2 + 2], wt[:, kc, o4 * 128:(o4 + 1) * 128], scT[:, kc, :], kc == 0, kc == 7, wk + ["scT"], [pk])
                tt(modT[:, 4 * jj:4 * jj + 4, :].rearrange("p a b -> p (a b)"), ps[:, 0:8], sm[:, 16 + 8 * jj:24 + 8 * jj], ALU.add,
                   [pk, "small"], [("modT", l)])
            pend = (cur, j) if cur is not None else None
            yield
        if 3 in blocks:
            stt(sc1[:].rearrange("p a b -> p (a b)"), modT[:, 8:16, :].rearrange("p a b -> p (a b)"), 1.0, sm[:, 0:16],
                ALU.add, ALU.mult, [("modT", l), "small"], [("sc1", l)])
        yield

    def layer_setup(l):
        sm = small[:, l, :]
        act(der[:, 0:4], sm[:, 114:118], AF.Exp, ["small"], ["der_a"], scale=-1.0)
        act(der[:, 4:8], der[:, 0:4], AF.Ln, ["der_a"], ["der_b"], bias=1.0)
        ts(der[:, 8:12], der[:, 4:8], -8.0, ALU.mult, ["der_b"], ["der_ca"])
        ts(der[:, 24:28], sm[:, 106:110], -1.0, ALU.mult, ["small"], ["der_nb"])
        ts(der[:, 28:32], sm[:, 110:114], -1.0, ALU.mult, ["small"], ["der_nb"])
        act(der[:, 12:16], sm[:, 119:123], AF.Exp, ["small"], ["der_es"])
        lam_init = 0.8 - 0.6 * math.exp(-0.3 * l)
        t1, t1k = Fp.get()
        tt(t1[:, 0:32], sm[:, 128:160], sm[:, 160:192], ALU.mult, ["small"], [t1k])
        tt(t1[:, 32:64], sm[:, 192:224], sm[:, 224:256], ALU.mult, ["small"], [t1k])
        S.op("dve", lambda e: e.tensor_reduce(out=der[:, 16:17], in_=t1[:, 0:32], axis=AX.X, op=ALU.add), reads=[t1k], writes=["der_l1"])
        S.op("dve", lambda e: e.tensor_reduce(out=der[:, 17:18], in_=t1[:, 32:64], axis=AX.X, op=ALU.add), reads=[t1k], writes=["der_l2"])
        act(der[:, 18:20], der[:, 16:18], AF.Exp, ["der_l1", "der_l2"], ["der_l3"])
        tt(der[:, 20:21], der[:, 19:20], der[:, 18:19], ALU.subtract, ["der_l3"], ["der_l4"])
        ts(der[:, 21:22], der[:, 20:21], -lam_init, ALU.add, ["der_l4"], ["der_nl"])
        ts(der[:, 22:23], sm[:, 118:119], 1.0 - lam_init, ALU.mult, ["small"], ["der_dg"])
        for c in range(2):
            for j in range(4):
                ts(cdiag[:, c * 4 + j, :], ident_b, sm[:, 96 + c * 4 + j:97 + c * 4 + j], ALU.mult, ["cstb", "small"], ["cdiag"])
        dma("pool", lruw[:], lruw_d[l], [], ["lruw"], "lruw")

    def qk_evac(P, ps, pk, dst_chunk, b, kind, l):
        lat = P["lat"]
        tl = 512 * b
        if kind == "plain" or (not lat and kind in ("ropeA", "ropeD")):
            act(qk[:, dst_chunk, tl:tl + 512], ps[:], AF.Copy, [pk], [("qk", dst_chunk, b)])
            return
        if not lat and kind == "aq":
            for m in range(2):
                ts(qk[:, dst_chunk + m, tl:tl + 512], ps[:], cstf[:, 128 + m:129 + m], ALU.mult, [pk, "cstf"], [("qk", dst_chunk + m, b)])
            return
        isA = kind in ("ropeA", "aq")
        Rm = cstb[:, C_RA:C_RA + 128] if isA else cstb[:, C_RD:C_RD + 128]
        cosT = cstb[:, (C_COSA if isA else C_COSD) + tl:(C_COSA if isA else C_COSD) + tl + 512]
        sinT = cstb[:, (C_SINA if isA else C_SIND) + tl:(C_SINA if isA else C_SIND) + tl + 512]
        yb, ybk = Bp.get()
        act(yb[:], ps[:], AF.Copy, [pk], [ybk])
        ps2, pk2 = PSp.get()
        mm(ps2[:], Rm, yb[:], True, True, [ybk, "cstb"], [pk2])
        t1, t1k = Fp.get()
        tt(t1[:], ps[:], cosT, ALU.mult, [pk, "cstb"], [t1k])
        t2, t2k = Fp.get()
        tt(t2[:], ps2[:], sinT, ALU.mult, [pk2, "cstb"], [t2k])
        if kind == "aq":
            tt(t1[:], t1[:], t2[:], ALU.add, [t1k, t2k], [t1k])
            for m in range(2):
                ts(qk[:, dst_chunk + m, tl:tl + 512], t1[:], cstf[:, 128 + m:129 + m], ALU.mult, [t1k, "cstf"], [("qk", dst_chunk + m, b)])
        else:
            tt(qk[:, dst_chunk, tl:tl + 512], t1[:], t2[:], ALU.add, [t1k, t2k], [("qk", dst_chunk, b)])

    def v_evac(ps, pk, Vt, name, tile, kindD):
        if not kindD:
            cp(Vt[:, tile, 0:64], ps[:, 0:64], [pk], [(name, tile)])
            act(Vt[:, tile, 128:256], ps[:, 64:192], AF.Copy, [pk], [(name, tile)])
            cp(Vt[:, tile, 320:384], ps[:, 192:256], [pk], [(name, tile)])
        else:
            cp(Vt[:, tile, 64:128], ps[:, 0:64], [pk], [(name, tile)])
            act(Vt[:, tile, 192:256], ps[:, 64:128], AF.Copy, [pk], [(name, tile)])

    def kv_out(P, ps, pk, ncols, tile, dst_ap_fn, l, dd=64):
        stg, sk = Fp.get()
        act(stg[:, 0:ncols], ps[:, 0:ncols], AF.Copy, [pk], [sk])
        bl = tile // 2
        s0 = (tile % 2) * 128
        dma("sp", dst_ap_fn(bl, s0), stg[:, 0:ncols].rearrange("p (a d) -> p a d", d=dd), [sk], [], ("Fd", sk[1]))

    def projections(P, l):
        gen = [None]
        tk = [0]

        def tick():
            tk[0] += 1
            if gen[0] is not None and tk[0] % 8 == 0:
                next(gen[0], None)

        lat = P["lat"]
        nb = P["T"] // 512
        ntile = P["T"] // 128
        wl = w_in[l]

        def blk(c0, n=512):
            return wl[:, c0:c0 + n].rearrange("(kc p) n -> p kc n", p=128)

        if lat:
            memset(bxp[:, :, 0:1], 0.0, ["bxp"])
            memset(bxp[:, :, 1025:1028], 0.0, ["bxp"])
        else:
            memset(bxp[:, :, 0:1], 0.0, ["bxp"])
            memset(bxp[:, :, 257:261], 0.0, ["bxp"])
            memset(bxp[:, :, 517:520], 0.0, ["bxp"])
        for j in range(2):
            wt, wk = load_w(blk(j * 512), cache=(l, j, lat))
            if j == 1 and not cst2_loaded:
                dma("pool", cstb[:, 768:NCB], cstb_d[:, 768:NCB], [], ["cstb2"], "su5")
                cst2_loaded.append(1)
            for o4 in range(4):
                for b in range(nb):
                    ps, pk = fm_proj(wt, wk, o4 * 128, P, b)
                    act(brg[:, j * 4 + o4, 512 * b:512 * b + 512], ps[:], AF.Silu, [pk], [("brg", j * 4 + o4, b)])
        wt, wk = load_w(blk(1536), cache=(l, 3, lat))
        for tile in range(ntile):
            ps, pk = tm_proj(wt, wk, 0, 256, tile)
            v_evac(ps, pk, VA, "VA", tile, False)
            if not lat:
                kv_out(P, ps, pk, 256, tile, lambda bl, s0: ndv[bl, l, :, s0:s0 + 128, :].rearrange("h s d -> s h d"), l)
        for c in range(2):
            for b in range(nb):
                ps, pk = fm_proj(wt, wk, 256 + c * 128, P, b)
                if lat:
                    cp(bxp[:, c, 1 + 512 * b:1 + 512 * b + 512], ps[:], [pk], ["bxp"])
                else:
                    cp(bxp[:, c, 1:257], ps[:, 0:256], [pk], ["bxp"])
                    cp(bxp[:, c, 261:517], ps[:, 256:512], [pk], ["bxp"])
        if not lat:
            gen[0] = lru(P, l)
        wt, wk = load_w(blk(1024), cache=(l, 2, lat))
        for c in range(2):
            for b in range(nb):
                ps, pk = fm_proj(wt, wk, c * 128, P, b)
                qk_evac(P, ps, pk, 2 * c, b, "aq", l)
                tick()
        for c in range(2):
            for b in range(nb):
                ps, pk = fm_proj(wt, wk, 256 + c * 128, P, b)
                qk_evac(P, ps, pk, 4 + c, b, "ropeA", l)
                tick()
        if not lat:
            for tile in range(ntile):
                ps, pk = tm_proj(wt, wk, 256, 256, tile)
                kv_out(P, ps, pk, 256, tile,
                       lambda bl, s0: ndk[bl, l, :, :, s0:s0 + 128, :].rearrange("h m s d -> s (h m) d"), l, dd=32)
        wt, wk = load_w(blk(2048), cache=(l, 4, lat))
        for c in range(4):
            for b in range(nb):
                ps, pk = fm_proj(wt, wk, c * 128, P, b)
                qk_evac(P, ps, pk, 6 + c, b, "plain", l)
                tick()
        if not lat:
            for tile in range(ntile):
                ps, pk = tm_proj(wt, wk, 256, 256, tile)
                kv_out(P, ps, pk, 256, tile, lambda bl, s0: nnk[bl, l, :, s0:s0 + 128, :].rearrange("h s d -> s h d"), l)
        wt, wk = load_w(blk(2560), cache=(l, 5, lat))
        for tile in range(ntile):
            ps, pk = tm_proj(wt, wk, 0, 256, tile)
            v_evac(ps, pk, VC, "VC", tile, False)
            tick()
            if not lat:
                kv_out(P, ps, pk, 256, tile, lambda bl, s0: nnv[bl, l, :, s0:s0 + 128, :].rearrange("h s d -> s h d"), l)
        for c in range(2):
            for b in range(nb):
                ps, pk = fm_proj(wt, wk, 256 + c * 128, P, b)
                qk_evac(P, ps, pk, 10 + c, b, "ropeD", l)
                tick()
        wt, wk = load_w(blk(3072, 256), 256, cache=(l, 6, lat))
        for b in range(nb):
            ps, pk = fm_proj(wt, wk, 0, P, b)
            qk_evac(P, ps, pk, 12, b, "ropeD", l)
            tl = 512 * b
            act(qk[64:128, 13, tl:tl + 512], qk[0:64, 12, tl:tl + 512], AF.Copy, [("qk", 12, b)], [("qk", 13, b)])
            act(qk[0:64, 13, tl:tl + 512], qk[64:128, 12, tl:tl + 512], AF.Copy, [("qk", 12, b)], [("qk", 13, b)])
        for tile in range(ntile):
            ps, pk = tm_proj(wt, wk, 128, 128, tile)
            v_evac(ps, pk, VD, "VD", tile, True)
            tick()
            if not lat:
                kv_out(P, ps, pk, 128, tile, lambda bl, s0: nsv[bl, l, :, s0:s0 + 128, :].rearrange("h s d -> s h d"), l)
        if not lat:
            for tile in range(ntile):
                ps, pk = tm_proj(wt, wk, 0, 128, tile)
                kv_out(P, ps, pk, 128, tile, lambda bl, s0: nsk[bl, l, :, s0:s0 + 128, :].rearrange("h s d -> s h d"), l)
        if gen[0] is not None:
            for _ in gen[0]:
                pass

    kst = hT[:, 7424:8192].bitcast(F32)

    def load_cache(l):
        fence("pre")
        voff = (0, 128, 192, 320)

        def srcs_A(c, j):
            return [(0, cdk_d[l, 2 * c:2 * c + 2, :, j * 128:(j + 1) * 128, :].rearrange("h m k d -> k (h m) d"), 128, 32)]

        def srcs_C(c, j):
            return [(0, cnk_d[l, 2 * c:2 * c + 2, j * 128:(j + 1) * 128, :].rearrange("h k d -> k h d"), 128, 64)]

        def srcs_D(c, j):
            if c == 0:
                return [(0, csk_d[l, :, j * 128:(j + 1) * 128, :].rearrange("h k d -> k h d"), 128, 64)]
            return [(0, csk_d[l, 1, j * 128:(j + 1) * 128, :], 64, None), (64, csk_d[l, 0, j * 128:(j + 1) * 128, :], 64, None)]

        for c in range(2):
            ps, pk = PSp.get()
            for j in range(4):
                stg, sk = Fp.get()
                for (dcol, src, n, dd) in srcs_A(c, j):
                    dv = stg[:, dcol:dcol + n]
                    if dd is not None:
                        dv = dv.rearrange("p (a d) -> p a d", d=dd)
                    dma("sp", dv, src, [], [sk], ("Fd", sk[1]))
                S.op("pe", lambda e, ps=ps, stg=stg, j=j: e.transpose(ps[:, j * 128:(j + 1) * 128], stg[:, 0:128], ident_f),
                     reads=[sk, "cstf"], writes=[pk])
            cp(cKA[:, c, :], ps[:], [pk, "cfence"], ["cKA" + str(c)])
        for h in range(4):
            dma("pool", cVA[:, :, voff[h]:voff[h] + 64], cdv_d[l, h].rearrange("(j p) d -> p j d", p=128), ["cfence"], [("cVA", h)], ("cva", h))
        for c0 in (64, 256):
            S.op("pool", lambda e, c0=c0: e.memset(cVA[:, :, c0:c0 + 64], 1.0), reads=["cfence"], writes=[("cVA", 4)])

        def rest():
            for h in range(4):
                dma("pool", cVC[:, :, voff[h]:voff[h] + 64], cnv_d[l, h].rearrange("(j p) d -> p j d", p=128), ["cfence"], [("cVC", h)], ("cvc", h))
            for kv in range(2):
                dma("pool", cVD[:, :, 64 + 128 * kv:128 + 128 * kv], csv_d[l, kv].rearrange("(j p) d -> p j d", p=128), ["cfence"], [("cVD", kv)], ("cvd", kv))
            for (v, name, cols, kk) in ((cVC, "cVC", (64, 256), 4), (cVD, "cVD", (0, 128, 256), 2)):
                for c0 in cols:
                    S.op("pool", lambda e, v=v, c0=c0: e.memset(v[:, :, c0:c0 + 64], 1.0), reads=["cfence"], writes=[(name, kk)])
            tiles = [(cKC, "cKC", c, j, srcs_C(c, j)) for c in range(2) for j in range(4)] + \
                    [(cKD, "cKD", c, j, srcs_D(c, j)) for c in range(2) for j in range(4)]

            def issue(n):
                (_, _, _, _, srcs) = tiles[n]
                si = n % 3
                for (dcol, src, nn, dd) in srcs:
                    dv = kst[:, si * 128 + dcol:si * 128 + dcol + nn]
                    if dd is not None:
                        dv = dv.rearrange("p (a d) -> p a d", d=dd)
                    dma("sp", dv, src, ["cfence"], [("kstg", si)], ("kd", si))

            issue(0)
            issue(1)
            yield
            for n in range(len(tiles)):
                (dst, name, c, j, _) = tiles[n]
                si = n % 3
                ps, pk = PSa.get()
                S.op("pe", lambda e, ps=ps, si=si: e.transpose(ps[:, 0:128], kst[:, si * 128:si * 128 + 128], ident_f),
                     reads=[("kstg", si), "cstf"], writes=[pk])
                cp(dst[:, c, j * 128:(j + 1) * 128], ps[:, 0:128], [pk, "cfence"], [name + str(c)])
                if n + 2 < len(tiles):
                    issue(n + 2)
                yield

        return rest()

    def lru(P, l, scr=None, pool=None, fine=False):
        lat = P["lat"]
        sm = small[:, l, :]
        nb = P["T"] // 512
        pool = pool or PSp
        for c in range(2):
            for b in range(nb):
                ps, pk = pool.get()
                if lat:
                    for j in range(4):
                        mm(ps[:], cdiag[:, c * 4 + j, :], bxp[:, c, 512 * b + j:512 * b + j + 512], j == 0, j == 3, ["cdiag", "bxp"], [pk])
                else:
                    for s_ in range(2):
                        for j in range(4):
                            mm(ps[:, 256 * s_:256 * s_ + 256], cdiag[:, c * 4 + j, :], bxp[:, c, 260 * s_ + j:260 * s_ + j + 256],
                               j == 0, j == 3, ["cdiag", "bxp"], [pk])
                act(xc[:, c, 512 * b:512 * b + 512], ps[:], AF.Identity, [pk, "small"], [("xc", c, b)], bias=sm[:, 104 + c:105 + c])
                yield
        segs = [(0, 512)] if lat else [(0, 256), (256, 256)]
        for d in range(2):
            order = list(range(nb)) if d == 0 else list(range(nb - 1, -1, -1))
            for bi, b in enumerate(order):
                CH = []
                for c in range(2):
                    xcs = xc[:, c, 512 * b:512 * b + 512]
                    psr, pkr = pool.get()
                    mm(psr[:], lruw[:, d * 4 + 0 + c, :], xcs, True, True, ["lruw", ("xc", c, b)], [pkr])
                    psi, pki = pool.get()
                    mm(psi[:], lruw[:, d * 4 + 2 + c, :], xcs, True, True, ["lruw", ("xc", c, b)], [pki])
                    if scr is None:
                        t3 = []
                        for _ in range(3):
                            t_, k_ = Fp.get()
                            t3.append((t_[:], [k_]))
                    else:
                        t3 = scr[3 * c:3 * c + 3]
                    (r, rk), (ii, ik), (a, ak) = t3
                    CH.append(dict(c=c, xcs=xcs, psr=psr, pkr=pkr, psi=psi, pki=pki, r=r, rk=rk, ii=ii, ik=ik, a=a, ak=ak,
                                   hcol=hlast[:, d * 2 + c:d * 2 + c + 1]))
                for q in CH:
                    c = q["c"]
                    act(q["r"], q["psr"][:], AF.Sigmoid, [q["pkr"], "small"], q["rk"], bias=sm[:, 106 + d * 2 + c:107 + d * 2 + c])
                    act(q["ii"], q["psi"][:], AF.Sigmoid, [q["pki"], "small"], q["ik"], bias=sm[:, 110 + d * 2 + c:111 + d * 2 + c])
                if fine:
                    yield
                for q in CH:
                    c = q["c"]
                    act(q["a"], q["r"], AF.Exp, q["rk"] + ["der_ca"], q["ak"], scale=der[:, 8 + d * 2 + c:9 + d * 2 + c])
                if fine:
                    yield
                for q in CH:
                    tt(q["r"], q["a"], q["a"], ALU.mult, q["ak"], q["rk"])
                    tt(q["ii"], q["ii"], q["xcs"], ALU.mult, q["ik"] + [("xc", q["c"], b)], q["ik"])
                if fine:
                    yield
                for q in CH:
                    act(q["r"], q["r"], AF.Ln, q["rk"], q["rk"], bias=1.0, scale=-1.0)
                for q in CH:
                    act(q["r"], q["r"], AF.Exp, q["rk"], q["rk"], scale=0.5)
                if fine:
                    yield
                for q in CH:
                    tt(q["ii"], q["ii"], q["r"], ALU.mult, q["ik"] + q["rk"], q["ik"])
                if fine:
                    yield
                for q in CH:
                    c = q["c"]
                    a, ak, ii, ik, hcol = q["a"], q["ak"], q["ii"], q["ik"], q["hcol"]
                    hs, hk = q["r"], q["rk"]
                    for (s0, sn) in segs:
                        if lat:
                            init = sm[:, 123 + d * 2 + c:124 + d * 2 + c] if bi == 0 else hcol
                            ireads = ["small"] if bi == 0 else [("hlast", d, c)]
                        else:
                            init = 0.0
                            ireads = []
                        if d == 0:
                            o_, a_, u_ = hs[:, s0:s0 + sn], a[:, s0:s0 + sn], ii[:, s0:s0 + sn]
                        else:
                            o_, a_, u_ = hs[:, s0:s0 + sn][:, ::-1], a[:, s0:s0 + sn][:, ::-1], ii[:, s0:s0 + sn][:, ::-1]
                        S.op("dve", lambda e, o_=o_, a_=a_, u_=u_, init=init: e.tensor_tensor_scan(
                            out=o_, data0=a_, data1=u_, initial=init, op0=ALU.mult, op1=ALU.add),
                            reads=ak + ik + ireads, writes=hk)
                        if lat:
                            if bi < nb - 1:
                                lastc = s0 + sn - 1 if d == 0 else s0
                                cp(hcol, hs[:, lastc:lastc + 1], hk, [("hlast", d, c)])
                        else:
                            lastc = s0 + sn - 1 if d == 0 else s0
                            sidx = s0 // 256
                            cp(stout[:, sidx * 4 + d * 2 + c:sidx * 4 + d * 2 + c + 1], hs[:, lastc:lastc + 1], hk, ["stout"])
                if fine:
                    yield
                for q in CH:
                    c = q["c"]
                    hs, hk = q["r"], q["rk"]
                    if d == 0:
                        cp(yrf[:, c, 512 * b:512 * b + 512], hs, hk, ["bxp"])
                    else:
                        tt(hs, hs, yrf[:, c, 512 * b:512 * b + 512], ALU.add, hk + ["bxp"], hk)
                        tt(brg[:, 2 + c, 512 * b:512 * b + 512], hs, brg[:, 2 + c, 512 * b:512 * b + 512], ALU.mult,
                           hk + [("brg", 2 + c, b)], [("brg", 2 + c, b)])
                yield
        if not lat:
            for bb in range(2):
                dma("sp", nst[bb, l, :, :].rearrange("d (c p) -> p d c", p=128),
                    stout[:, bb * 4:bb * 4 + 4].rearrange("p (d c) -> p d c", d=2), ["stout"], [], "ost%d" % bb)

    GRP = 4

    strip_pref = {}
    carry = []

    def run_jobs(jobs, hook=None, hook_every=6, keep_tail=False):
        items = []
        for a in range(0, len(jobs), 2):
            ja, jb = jobs[a], jobs[a + 1]
            assert len(ja["keys"]) == len(jb["keys"])
            for i in range(len(ja["keys"])):
                items.append((a, i))
                items.append((a + 1, i))
        groups = [items[a:a + GRP] for a in range(0, len(items), GRP)]
        acc = {}
        prev = None
        deferred = [(3, f) for f in carry]
        del carry[:]
        for g in range(len(groups) + 1):
            cur = None
            for dfn in [f for (ga, f) in deferred if ga <= g]:
                dfn()
            deferred = [(ga, f) for (ga, f) in deferred if ga > g]
            if hook is not None and g % hook_every == (1 if hook_every > 1 else 0):
                hook()
            if g < len(groups):
                cur = []
                tiles = []
                for (j, i) in groups[g]:
                    job = jobs[j]
                    if i == 0 and job.get("pre") is not None:
                        job["pre"]()
                    kd = job["keys"][i]
                    Nq = job["Nq"]
                    pss, pks = PSa.get()
                    ex = kd.get("extra", [])
                    mm(pss[:, 0:Nq], kd["kT"], job["q_ap"], True, len(ex) == 0, kd["kreads"] + job["qreads"], [pks])
                    tiles.append((pss, pks, ex, Nq, job))
                for xi in range(2):
                    for (pss, pks, ex, Nq, job) in tiles:
                        if xi < len(ex):
                            xl, xr, xreads = ex[xi]
                            mm(pss[:, 0:Nq], xl, xr, False, xi == len(ex) - 1, xreads, [pks])
                for n_, (j, i) in enumerate(groups[g]):
                    (pss, pks, ex, Nq, job) = tiles[n_]
                    pt, ptk = Bpt.get()
                    act(pt[:, 0:Nq], pss[:, 0:Nq], AF.Exp, [pks], [ptk], scale=job["scale"])
                    cur.append((j, i, pt, ptk))
            if prev is not None:
                allk = [ptk for (_, _, _, ptk) in prev]
                for n_, (j, i, pt, ptk) in enumerate(prev):
                    job = jobs[j]
                    kd = job["keys"][i]
                    Nq = job["Nq"]
                    if i == 0:
                        acc[j] = PSo.get()
                    pso, pko = acc[j]
                    nk = len(job["keys"])
                    mm(pso[:, 0:Nq], kd["v"], pt[:, 0:Nq], i == 0, i == nk - 1, kd["vreads"] + (allk if n_ == 0 else [ptk]), [pko])
                    if i == nk - 1:
                        later = job["epi"](pso, pko)
                        if later is not None:
                            deferred.append((g + 5, later))
                        del acc[j]
            prev = cur
        if keep_tail:
            carry.extend(f for (ga, f) in deferred)
        else:
            for (ga, f) in deferred:
                f()

    def norm_out(pso, pko, Nq, odd, add_es=None, on_act=False):
        o_lo, d_lo = (64, 0) if odd else (0, 64)
        rc, rck = Fp.get()
        if on_act:
            if add_es is not None:
                act(rc[o_lo:o_lo + 64, 0:Nq], pso[d_lo:d_lo + 64, 0:Nq], AF.Ln, [pko, "der_es"], [rck], bias=der[d_lo:d_lo + 64, add_es:add_es + 1])
            else:
                act(rc[o_lo:o_lo + 64, 0:Nq], pso[d_lo:d_lo + 64, 0:Nq], AF.Ln, [pko], [rck])
            act(rc[o_lo:o_lo + 64, 0:Nq], rc[o_lo:o_lo + 64, 0:Nq], AF.Exp, [rck], [rck], scale=-1.0)
        elif add_es is not None:
            ts(rc[o_lo:o_lo + 64, 0:Nq], pso[d_lo:d_lo + 64, 0:Nq], der[d_lo:d_lo + 64, add_es:add_es + 1], ALU.add, [pko, "der_es"], [rck])
            recip(rc[o_lo:o_lo + 64, 0:Nq], rc[o_lo:o_lo + 64, 0:Nq], [rck], [rck])
        else:
            recip(rc[o_lo:o_lo + 64, 0:Nq], pso[d_lo:d_lo + 64, 0:Nq], [pko], [rck])
        tt(rc[o_lo:o_lo + 64, 0:Nq], pso[o_lo:o_lo + 64, 0:Nq], rc[o_lo:o_lo + 64, 0:Nq], ALU.mult, [pko, rck], [rck])
        return rc, rck, o_lo

    def q_blocks(P):
        if P["lat"]:
            return [(0, 512, 0, 0), (512, 512, 1, 0)]
        return [(0, 256, 0, 0), (256, 256, 0, 1)]

    def new_keys(P, seq):
        if P["lat"]:
            return [(128 * j, j) for j in range(8)]
        return [(256 * seq + 128 * j, 2 * seq + j) for j in range(2)]

    offA = (0, 64, 192, 256)

    def attn_A(P, l, hook=None, hook_every=6):
        lat = P["lat"]
        scale = 32.0 ** -0.5
        jobs = []
        for c in range(2):
            for (tl, Nq, bidx, seq) in q_blocks(P):
                grp = {}

                def finish_group(grp=grp, c=c, tl=tl, Nq=Nq, bidx=bidx):
                    dch, dk_ = grp["dch"]
                    sq, sqk = Bp.get()
                    tt(sq[:, 0:Nq], dch[:, 0:Nq], dch[:, 0:Nq], ALU.mult, [dk_], [sqk])
                    psn, pkn = PSa.get()
                    mm(psn[:, 0:Nq], bones_b, sq[:, 0:Nq], True, True, [sqk, "cstb"], [pkn])
                    rs, rsk = Fp.get()
                    act(rs[:, 0:Nq], psn[:, 0:Nq], AF.Ln, [pkn, "cstf"], [rsk], bias=eps_c, scale=1.0 / 64)
                    act(rs[:, 0:Nq], rs[:, 0:Nq], AF.Exp, [rsk], [rsk], scale=-0.5)
                    stt(dch[:, 0:Nq], dch[:, 0:Nq], der[:, 22:23], rs[:, 0:Nq], ALU.mult, ALU.mult, [dk_, rsk, "der_dg"], [dk_])
                    tt(brg[:, c, tl:tl + Nq], dch[:, 0:Nq], brg[:, c, tl:tl + Nq], ALU.mult, [dk_, ("brg", c, bidx)], [("brg", c, bidx)])

                for m in range(2):
                    for hh in range(2):
                        h = 2 * c + hh
                        rows = slice(64 * hh, 64 * hh + 64)
                        kl = []
                        for (ko, tile) in new_keys(P, seq):
                            kl.append(dict(kT=qk[rows, 4 + c, ko:ko + 128], kreads=[("qk", 4 + c, ko // 512)],
                                           v=VA[:, tile, offA[h]:offA[h] + 128], vreads=[("VA", tile)]))
                        if lat:
                            for j in range(4):
                                kl.append(dict(kT=cKA[rows, c, 128 * j:128 * j + 128], kreads=["cKA%d" % c],
                                               v=cVA[:, j, offA[h]:offA[h] + 128], vreads=[("cVA", x) for x in range(5)]))

                        def epi(pso, pko, grp=grp, hh=hh, m=m, Nq=Nq, fin=finish_group):
                            if "dch" not in grp:
                                grp["dch"] = Fp.get()
                            dch, dk_ = grp["dch"]
                            r = norm_out(pso, pko, Nq, hh == 1, on_act=True)
                            if m == 0:
                                grp["r0", hh] = r
                            else:
                                (r0, r0k, o_lo) = grp["r0", hh]
                                (r1, r1k, _) = r
                                stt(dch[o_lo:o_lo + 64, 0:Nq], r1[o_lo:o_lo + 64, 0:Nq], der[o_lo:o_lo + 64, 21:22], r0[o_lo:o_lo + 64, 0:Nq],
                                    ALU.mult, ALU.add, [r0k, r1k, "der_nl"], [dk_])
                                if hh == 1:
                                    if lat:
                                        return fin
                                    fin()

                        jobs.append(dict(qreads=[("qk", 2 * c + m, bidx)], q_ap=qk[rows, 2 * c + m, tl:tl + Nq], keys=kl,
                                         scale=scale, Nq=Nq, epi=epi))
        run_jobs(jobs, hook=hook, hook_every=hook_every, keep_tail=lat)

    def attn_C(P, l, hook=None):
        lat = P["lat"]
        jobs = []
        for c in range(2):
            hs_ = [{}, {}]
            for qi, (tl, Nq, bidx, seq) in enumerate(q_blocks(P)):
                for hh in range(2):
                    h = 2 * c + hh
                    rows = slice(64 * hh, 64 * hh + 64)

                    def pre(hsd=hs_[hh], h=h):
                        if (l, h) in strip_pref:
                            hsd["s"] = strip_pref.pop((l, h))
                            return
                        sp_, spk = strip_rot.get()
                        dma("pool", sp_[:], strips_d[l, h], [], [spk], spk)
                        hsd["s"] = (sp_, spk)

                    kl = []
                    if lat:
                        for j in range(4):
                            kl.append(dict(kT=cKC[rows, c, 128 * j:128 * j + 128], kreads=["cKC%d" % c],
                                           v=cVC[:, j, offA[h]:offA[h] + 128], vreads=[("cVC", x) for x in range(5)]))
                        qb = bidx
                        for kc in (range(0, 6) if qb == 0 else range(2, 8)):
                            e0 = 10 - 2 * kc + 8 * qb
                            kl.append(dict(kT=qk[rows, 8 + c, 128 * kc:128 * kc + 128], kreads=[("qk", 8 + c, kc // 4)],
                                           v=VC[:, kc, offA[h]:offA[h] + 128], vreads=[("VC", kc)], lazy=(hs_[hh], e0, kc, qb)))
                    else:
                        for (ko, tile) in new_keys(P, seq):
                            kl.append(dict(kT=qk[rows, 8 + c, ko:ko + 128], kreads=[("qk", 8 + c, ko // 512)],
                                           v=VC[:, tile, offA[h]:offA[h] + 128], vreads=[("VC", tile)]))

                    def epi(pso, pko, c=c, hh=hh, tl=tl, Nq=Nq, bidx=bidx):
                        rc, rck, o_lo = norm_out(pso, pko, Nq, hh == 1, on_act=True)
                        tt(brg[o_lo:o_lo + 64, 4 + c, tl:tl + Nq], rc[o_lo:o_lo + 64, 0:Nq], brg[o_lo:o_lo + 64, 4 + c, tl:tl + Nq], ALU.mult,
                           [rck, ("brg", 4 + c, bidx)], [("brg", 4 + c, bidx)])

                    jobs.append(dict(qreads=[("qk", 6 + c, bidx)], q_ap=qk[rows, 6 + c, tl:tl + Nq], keys=kl, scale=0.125, Nq=Nq, epi=epi,
                                     pre=(pre if (lat and qi == 0) else None)))
        run_jobs_lazy(jobs, hook)

    def run_jobs_lazy(jobs, hook=None):
        for job in jobs:
            opre = job.get("pre")

            def pre2(job=job, opre=opre):
                if opre is not None:
                    opre()
                for kd in job["keys"]:
                    if "lazy" in kd:
                        hs_, e0, kc, qb = kd["lazy"]
                        sp_, spk = hs_["s"]
                        kd["extra"] = [(i8_b, sp_[:, e0 * 64:e0 * 64 + 512], [spk, "cstb"]),
                                       (cstb[0:16, C_MA + 128 * kc:C_MA + 128 * kc + 128],
                                        cstb[0:16, C_MB + 512 * qb:C_MB + 512 * qb + 512], ["cstb"])]
            job["pre"] = pre2
        run_jobs(jobs, hook=hook, hook_every=1)

    def attn_D(P, l, hook=None):
        lat = P["lat"]
        jobs = []
        for kv in range(2):
            for (tl, Nq, bidx, seq) in q_blocks(P):
                for g in range(2):
                    qh = 2 * kv + g
                    rows = slice(64 * g, 64 * g + 64)
                    kchunk = 12 if kv == g else 13
                    ccol = 0 if kv == g else 1
                    vsl = (64 + 128 * kv) if g == 0 else (128 * kv)
                    kl = []
                    if lat:
                        for j in range(4):
                            kl.append(dict(kT=cKD[rows, ccol, 128 * j:128 * j + 128], kreads=["cKD%d" % ccol],
                                           v=cVD[:, j, vsl:vsl + 128], vreads=[("cVD", x) for x in range(3)]))
                        qb = bidx
                        for kc in (range(0, 5) if qb == 0 else range(3, 8)):
                            x0 = 512 * qb - 128 * kc + 512
                            ex = [(ident_b, cstb[:, C_BAND + x0:C_BAND + x0 + 512], ["cstb"])]
                            kl.append(dict(kT=qk[rows, kchunk, 128 * kc:128 * kc + 128], kreads=[("qk", kchunk, kc // 4)],
                                           v=VD[:, kc, vsl:vsl + 128], vreads=[("VD", kc)], extra=ex))
                    else:
                        for (ko, tile) in new_keys(P, seq):
                            kl.append(dict(kT=qk[rows, kchunk, ko:ko + 128], kreads=[("qk", kchunk, ko // 512)],
                                           v=VD[:, tile, vsl:vsl + 128], vreads=[("VD", tile)]))

                    def epi(pso, pko, kv=kv, g=g, qh=qh, tl=tl, Nq=Nq, bidx=bidx):
                        rc, rck, o_lo = norm_out(pso, pko, Nq, g == 1, add_es=12 + qh, on_act=True)
                        tt(brg[o_lo:o_lo + 64, 6 + kv, tl:tl + Nq], rc[o_lo:o_lo + 64, 0:Nq], brg[o_lo:o_lo + 64, 6 + kv, tl:tl + Nq], ALU.mult,
                           [rck, ("brg", 6 + kv, bidx)], [("brg", 6 + kv, bidx)])

                    jobs.append(dict(qreads=[("qk", 10 + kv, bidx)], q_ap=qk[rows, 10 + kv, tl:tl + Nq], keys=kl, scale=0.125, Nq=Nq, epi=epi))
        run_jobs(jobs, hook=hook, hook_every=1)

    strip_rot = Rot("strip", stripb)

    def merge_and_out(P, l, between=None):
        nb = P["T"] // 512
        sm = small[:, l, :]
        g = P["g"]
        for oc in range(8):
            wt, wk = load_w(lambda n, oc=oc: w_mg[l][:, n, oc * 128:(oc + 1) * 128].rearrange("(kc p) m -> p kc m", p=128), four=True,
                            cache=(l, 7 + oc, P["lat"]))
            wo_, wok = wbo_rot.get()
            if P["lat"]:
                dma("pool", wo_[:], w_bo[l][:, :, oc * 128:(oc + 1) * 128].rearrange("n (kc p) m -> p (n kc) m", p=128), [], [wok], wok)
                dma("sp", wsc_bo[l, oc], wo_[:], [wok], [("wscbo", l, oc)], ("wsto", wok[1]))
            else:
                dma("pool", wo_[:], wsc_bo[l, oc], [("wscbo", l, oc)], [wok], wok)
            for b in range(nb):
                macc, mk = Fp.get()
                for n in range(4):
                    psg, pkg = PSp.get()
                    for kc in range(8):
                        mm(psg[:], wt[:, kc, n * 128:(n + 1) * 128], hsl(kc, 512 * b, 512), kc == 0, kc == 7, wk + [("hT", kc * 2 + b)], [pkg])
                    psp, pkp = PSp.get()
                    for k2 in range(2):
                        mm(psp[:], wo_[:, n * 2 + k2, :], brg[:, 2 * n + k2, 512 * b:512 * b + 512], k2 == 0, k2 == 1,
                           [wok, ("brg", 2 * n + k2, b)], [pkp])
                    gt, gk = Fp.get()
                    act(gt[:], psg[:], AF.Sigmoid, [pkg, "small"], [gk], bias=sm[:, 64 + n * 8 + oc:65 + n * 8 + oc])
                    if n == 0:
                        tt(macc[:], psp[:], gt[:], ALU.mult, [pkp, gk], [mk])
                    else:
                        tt(gt[:], psp[:], gt[:], ALU.mult, [pkp, gk], [gk])
                        if n < 3:
                            tt(macc[:], macc[:], gt[:], ALU.add, [mk, gk], [mk])
                        else:
                            tt(qk[:, oc, 512 * b:512 * b + 512], macc[:], gt[:], ALU.add, [mk, gk], [("qk", oc, b)])
        if between is not None:
            between()
        for j in range(2):
            wt, wk = load_w(w_o[l][:, j * 512:(j + 1) * 512].rearrange("(kc p) n -> p kc n", p=128), cache=(l, 15 + j, P["lat"]))
            for o4 in range(4):
                oc = 4 * j + o4
                for b in range(nb):
                    tok0 = P["t0"] + 512 * b
                    ps, pk = PSp.get()
                    for kc in range(8):
                        mm(ps[:], wt[:, kc, o4 * 128:(o4 + 1) * 128], qk[:, kc, 512 * b:512 * b + 512], kc == 0, kc == 7, wk + [("qk", kc, b)], [pk])
                    stt(xT[:, oc, tok0:tok0 + 512], ps[:], modTs[l][:, 16 + oc, g:g + 1], xT[:, oc, tok0:tok0 + 512], ALU.mult, ALU.add,
                        [pk, ("modT", l), ("xT", oc, tok0 // 512)], [("xT", oc, tok0 // 512)])

    passes = [dict(name="L", t0=0, T=LT, g=0, lat=True), dict(name="P", t0=LT, T=PT, g=1, lat=False)]
    stage = [0]

    def go():
        stage[0] += 1
        return STOP is None or stage[0] <= STOP

    def load_x(mg):
        for tt_i in range(NT // 128):
            blk3 = tt_i // 4
            if tt_i % 2 == 1:
                next(mg, None)
            for half in range(2):
                stg, sk = Fp.get()
                dma("sp", stg[:], x_all[tt_i * 128:(tt_i + 1) * 128, half * 512:(half + 1) * 512], [], [sk], ("Fd", sk[1]))
                ps, pk = PSp.get()
                for j in range(4):
                    S.op("pe", lambda e, ps=ps, stg=stg, j=j: e.transpose(ps[:, j * 128:(j + 1) * 128], stg[:, j * 128:(j + 1) * 128], ident_f),
                         reads=[sk, "cstf"], writes=[pk])
                dst = xT[:, half * 4:half * 4 + 4, tt_i * 128:(tt_i + 1) * 128]
                src = ps[:].rearrange("p (a b) -> p a b", a=4)
                keys = [("xT", half * 4 + j, blk3) for j in range(4)]
                if (tt_i + half) % 2 == 0:
                    act(dst, src, AF.Copy, [pk], keys)
                else:
                    cp(dst, src, [pk], keys)


    modg = {}
    h_done = {}
    cst2_loaded = []
    for l in range(DEPTH):
        if not go():
            break
        if l == 0:
            mg0 = mod_gen(0, PSp, blocks=(0, 1, 2, 3))
            load_x(mg0)
            for _ in mg0:
                pass
        layer_setup(l)
        for P in passes:
            nb = P["T"] // 512
            if not go():
                break
            if not h_done.get((l, P["name"])):
                for b in range(nb):
                    emit_h(P, b, l)
            if not go():
                break
            projections(P, l)
            if not go():
                break
            if not go():
                break
            if P["lat"]:
                cg = load_cache(l)
                scr = [(qk[:, k, :].bitcast(F32), [("qk", k, 0), ("qk", k, 1)]) for k in range(6)]
                lg = lru(P, l, scr=scr, pool=PSa, fine=True)
                for _ in range(4):
                    next(lg, None)
                for h_ in range(2):
                    sp_, spk = strip_rot.get()
                    dma("pool", sp_[:], strips_d[l, h_], [], [spk], spk)
                    strip_pref[(l, h_)] = (sp_, spk)
                gens = ([mod_gen(0, PSa, blocks=(4, 5))] if l == 0 else []) + ([mod_gen(l + 1, PSa)] if l + 1 < DEPTH else [])

                def chain(gs):
                    for g_ in gs:
                        for _ in g_:
                            yield

                mgn = chain(gens)
                hc = [0]

                def ahook(cg=cg, mgn=mgn, hc=hc):
                    hc[0] += 1
                    if hc[0] % 4 == 2:
                        next(mgn, None)
                    else:
                        next(cg, None)

                attn_A(P, l, hook=ahook, hook_every=1)
                for _ in cg:
                    pass
                for _ in mgn:
                    pass
            else:
                attn_A(P, l)
            if not go():
                break
            if P["lat"]:
                lhook = lambda: next(lg, None)
            else:
                lg, lhook = None, None
            attn_C(P, l, hook=lhook)
            if not go():
                break
            attn_D(P, l, hook=lhook)
            if lg is not None:
                for _ in lg:
                    pass
            if not go():
                break
            if P["lat"]:
                fence("post")
            if P["lat"]:
                for b in range(nb):
                    emit_h(P, b, l, lnexp=True)
            if P["lat"]:
                nxt = (passes[1], l)
            else:
                nxt = (passes[0], l + 1) if l + 1 < DEPTH else None

            def between(nxt=nxt):
                if nxt is None or STOP is not None:
                    return
                Pn, ln = nxt
                for b in range(Pn["T"] // 512):
                    emit_h(Pn, b, ln)
                h_done[(ln, Pn["name"])] = True

            merge_and_out(P, l, between=between)
        if STOP is not None and stage[0] > STOP:
            break

    if STOP is not None:
        dbg_brg = dout("dbg_brg", [128, 8, 1024])
        dbg_qk = dout("dbg_qk", [128, 14, 1024])
        dbg_xT = dout("dbg_xT", [128, 8, NT])
        dbg_xc = dout("dbg_xc", [128, 2, 1024])
        dma("pool", dbg_brg, brg[:], [("brg", c, b) for c in range(8) for b in range(2)], [], "dbg0")
        dma("pool", dbg_qk, qk[:], [("qk", c, b) for c in range(14) for b in range(2)], [], "dbg1")
        dma("sp", dbg_xT, xT[:], [("xT", c, b) for c in range(8) for b in range(3)], [], "dbg2")
        dma("pool", dbg_xc, xc[:], [("xc", c, b) for c in range(2) for b in range(2)], [], "dbg3")
    for blk3 in range(NT // 512):
        tok0 = blk3 * 512
        r, rk = rms_rstd(tok0, blk3)
        for kc in range(8):
            yt, yk = Fp.get()
            stt(yt[:], xT[:, kc, tok0:tok0 + 512], normf[:, kc:kc + 1], r[:], ALU.mult, ALU.mult, [("xT", kc, blk3), "normf", rk], [yk])
            ps, pk = PSp.get()
            for t4 in range(4):
                S.op("pe", lambda e, ps=ps, yt=yt, t4=t4: e.transpose(ps[:, t4 * 128:(t4 + 1) * 128], yt[:, t4 * 128:(t4 + 1) * 128], ident_f),
                     reads=[yk, "cstf"], writes=[pk])
            og, ok = Fp.get()
            if kc % 2 == 0:
                act(og[:], ps[:], AF.Copy, [pk], [ok])
            else:
                cp(og[:], ps[:], [pk], [ok])
            dma("sp", y_all[tok0:tok0 + 512, kc * 128:(kc + 1) * 128].rearrange("(t p) f -> p t f", p=128),
                og[:].rearrange("p (t f) -> p t f", t=4), [ok], [], ("Fd", ok[1]))

    with nc.allow_non_contiguous_dma(reason="small strided cache/state layouts"):
        S.emit(nc, st)
    return nc, st, S


def _host_constants():
    cb = np.zeros((128, NCB), np.float32)
    cf = np.zeros((128, NCF), np.float32)
    eye = np.eye(128, dtype=np.float32)
    cb[:, C_ID:C_ID + 128] = eye
    cb[:, C_I8:C_I8 + 128] = 8.0 * eye
    cb[:, C_ONES:C_ONES + 128] = 1.0
    p = np.arange(128)
    cb[:, C_BONES:C_BONES + 128] = (p[:, None] // 64 == p[None, :] // 64).astype(np.float32)
    t = np.arange(LT)
    row, col = t // 64, t % 64
    RA = np.zeros((128, 128), np.float32)
    RD = np.zeros((128, 128), np.float32)
    for pp in range(128):
        j = pp % 32
        jj = j % 16
        i = jj % 8
        inv = 10000.0 ** (-(i / 8.0))
        pos = row if j < 16 else col
        ang = pos.astype(np.float32) * np.float32(inv)
        cb[pp, C_COSA:C_COSA + LT] = np.cos(ang)
        if jj < 8:
            cb[pp, C_SINA:C_SINA + LT] = -np.sin(ang)
            RA[pp + 8, pp] = 1.0
        else:
            cb[pp, C_SINA:C_SINA + LT] = np.sin(ang)
            RA[pp - 8, pp] = 1.0
        j = pp % 64
        jj = j % 32
        i = jj % 16
        inv = 10000.0 ** (-(i / 16.0))
        pos = row if j < 32 else col
        ang = pos.astype(np.float32) * np.float32(inv)
        cb[pp, C_COSD:C_COSD + LT] = np.cos(ang)
        if jj < 16:
            cb[pp, C_SIND:C_SIND + LT] = -np.sin(ang)
            RD[pp + 16, pp] = 1.0
        else:
            cb[pp, C_SIND:C_SIND + LT] = np.sin(ang)
            RD[pp - 16, pp] = 1.0
    cb[:, C_RA:C_RA + 128] = RA
    cb[:, C_RD:C_RD + 128] = RD
    x = np.arange(1152)
    cb[:, C_BAND:C_BAND + 1152] = np.where(np.abs(x[None, :] - 512 - p[:, None]) <= 128, 0.0, MASKV)
    rows = 16
    row_start = np.clip(np.arange(rows) - 4, 0, rows - 8)
    krow = np.arange(LT) // 64
    for r in range(16):
        ok = (krow >= row_start[r]) & (krow < row_start[r] + 8)
        cb[r, C_MA:C_MA + LT] = np.where(ok, 0.0, MASKV)
        cb[r, C_MB:C_MB + LT] = (krow == r).astype(np.float32)
    cf[:, 0:128] = eye
    cf[:, 128] = ((p % 64) < 32).astype(np.float32)
    cf[:, 129] = ((p % 64) >= 32).astype(np.float32)
    cf[:, 130] = EPS
    return cb, cf


def _strips(na_rpb):
    out = np.zeros((DEPTH, 4, 128, 22, 64), np.float32)
    qcol = np.arange(64)
    kcol = np.arange(64)
    col_start = np.clip(qcol - 8, 0, 48)
    ok = (kcol[:, None] >= col_start[None, :]) & (kcol[:, None] < col_start[None, :] + 16)
    dcol = np.clip(kcol[:, None] - qcol[None, :] + 15, 0, 30)
    for krl in range(2):
        for e in range(22):
            d = 17 - e + krl
            if 0 <= d <= 14:
                blk = na_rpb[:, :, d, :][:, :, dcol]
                blk = np.where(ok[None, None], blk, np.float32(-1000.0))
                out[:, :, krl * 64:(krl + 1) * 64, e, :] = blk
    return out.reshape(DEPTH, 4, 128, NSTRIP)


_CACHE = {}


def kernel(x_prompt, x_sample, cache_diff_k, cache_diff_v, cache_na_k, cache_na_v, cache_swa_k, cache_swa_v,
           state_lru, c, c_ctx, norm_g, w_ada, b_ada, w_in, diff_lambda, diff_norm_g, conv_w, conv_b,
           lru_wa, lru_ba, lru_wx, lru_bx, lru_lam, na_rpb, swa_sink, w_mg, b_mg, w_bo, w_o, norm_f):
    f = lambda a: np.ascontiguousarray(np.asarray(a, dtype=np.float32))
    (x_prompt, x_sample, cache_diff_k, cache_diff_v, cache_na_k, cache_na_v, cache_swa_k, cache_swa_v, state_lru, c, c_ctx,
     norm_g, w_ada, b_ada, w_in, diff_lambda, diff_norm_g, conv_w, conv_b, lru_wa, lru_ba, lru_wx, lru_bx, lru_lam, na_rpb,
     swa_sink, w_mg, b_mg, w_bo, w_o, norm_f) = map(f, (
        x_prompt, x_sample, cache_diff_k, cache_diff_v, cache_na_k, cache_na_v, cache_swa_k, cache_swa_v, state_lru, c, c_ctx,
        norm_g, w_ada, b_ada, w_in, diff_lambda, diff_norm_g, conv_w, conv_b, lru_wa, lru_ba, lru_wx, lru_bx, lru_lam, na_rpb,
        swa_sink, w_mg, b_mg, w_bo, w_o, norm_f))
    if "nc" not in _CACHE:
        _CACHE["nc"] = build_program()
    nc, _st, S = _CACHE["nc"]
    cb, cf = _host_constants()
    strips = _strips(na_rpb)

    def fm(v, nch):
        return v.reshape(nch, 128).T

    lruw = np.zeros((DEPTH, 128, 8, 128), np.float32)
    for l in range(DEPTH):
        for d in range(2):
            for kind, w in enumerate((lru_wa, lru_wx)):
                for cc in range(2):
                    for hb in range(2):
                        lruw[l, hb * 64:(hb + 1) * 64, d * 4 + kind * 2 + cc, hb * 64:(hb + 1) * 64] = w[l, d, 2 * cc + hb]
    normf = fm(norm_f, 8)
    in_maps = []
    for core in range(NCORES):
        small = np.zeros((128, DEPTH, NS), np.float32)
        for l in range(DEPTH):
            s = small[:, l, :]
            s[:, 0:16] = np.repeat(fm(norm_g[l], 8), 2, axis=1)
            s[:, 16:64] = np.repeat(fm(b_ada[l], 24), 2, axis=1)
            for n in range(4):
                s[:, 64 + n * 8:72 + n * 8] = fm(b_mg[l, n], 8)
            for cc in range(2):
                for j in range(4):
                    s[:, 96 + cc * 4 + j] = conv_w[l, j, cc * 128:(cc + 1) * 128]
                s[:, 104 + cc] = conv_b[l, cc * 128:(cc + 1) * 128]
                for d in range(2):
                    s[:, 106 + d * 2 + cc] = lru_ba[l, d, cc * 128:(cc + 1) * 128]
                    s[:, 110 + d * 2 + cc] = lru_bx[l, d, cc * 128:(cc + 1) * 128]
                    s[:, 114 + d * 2 + cc] = lru_lam[l, d, cc * 128:(cc + 1) * 128]
                    s[:, 123 + d * 2 + cc] = state_lru[core, l, d, cc * 128:(cc + 1) * 128]
            s[:, 118] = np.tile(diff_norm_g[l], 2)
            s[:, 119:123] = swa_sink[l][None, :]
            s[:, 128:256] = diff_lambda[l].reshape(1, 128)
        cT = np.stack([fm(c[core], 8), fm(c_ctx, 8)], axis=2)
        x_all = np.concatenate([x_sample[core], x_prompt[2 * core], x_prompt[2 * core + 1]], axis=0)
        in_maps.append(dict(
            x_all=np.ascontiguousarray(x_all), w_ada=w_ada, w_in=w_in, w_mg=w_mg, w_bo=w_bo, w_o=w_o,
            small=small, cT=np.ascontiguousarray(cT), normf=np.ascontiguousarray(normf), lruw=lruw, strips=strips,
            cstb=cb, cstf=cf, cdk=cache_diff_k[core], cdv=cache_diff_v[core], cnk=cache_na_k[core], cnv=cache_na_v[core],
            csk=cache_swa_k[core], csv=cache_swa_v[core]))
    res = run_bass_kernel_spmd(nc, in_maps, core_ids=list(range(NCORES)))
    R = res.results
    y_sample = np.stack([R[i]["y_all"][:LT] for i in range(NCORES)], axis=0)
    y_prompt = np.concatenate([R[i]["y_all"][LT:].reshape(2, 256, D) for i in range(NCORES)], axis=0)
    cat = lambda k: np.concatenate([R[i][k] for i in range(NCORES)], axis=0)
    if STOP is not None:
        _CACHE["dbg"] = dict(brg=R[0]["dbg_brg"], qk=R[0]["dbg_qk"], xT=R[0]["dbg_xT"], xc=R[0]["dbg_xc"])
    return (y_prompt, y_sample, cat("ndk"), cat("ndv"), cat("nnk"), cat("nnv"), cat("nsk"), cat("nsv"), cat("nst"))
```

```python
import math
import numpy as np
from contextlib import ExitStack
import concourse.bass as bass
import concourse.mybir as mybir
from concourse.bass_utils import run_bass_kernel_spmd

F32 = mybir.dt.float32
BF16 = mybir.dt.bfloat16
ALU = mybir.AluOpType
AF = mybir.ActivationFunctionType
AX = mybir.AxisListType

NCORES = 8
D = 1024
DEPTH = 2
LT = 1024
PT = 512
NT = LT + PT
PAST = 512
EPS = 1e-6
MASKV = -30000.0
NS = 256
ENGS = ("pe", "act", "dve", "pool", "sp")

C_ID, C_I8, C_ONES, C_BONES, C_RA, C_RD = 0, 128, 256, 384, 512, 640
C_COSA, C_SINA, C_COSD, C_SIND = 768, 1792, 2816, 3840
C_BAND, C_MA, C_MB = 4864, 6016, 7040
NCB = 8064
NCF = 132
NSTRIP = 22 * 64
STOP = None


class Sched:
    def __init__(self):
        self.ops = []
        self.last_writer = {}
        self.readers = {}
        self.dma_keys = []

    def op(self, eng, fn, reads=(), writes=(), dma=None):
        i = len(self.ops)
        deps = set()
        px = [r for r in reads if isinstance(r, tuple) and r[0] in ("PS", "PO")]
        if px:
            reads = [r for r in reads if r not in px]
            writes = list(writes) + [r for r in px if r not in writes]
        for r in reads:
            w = self.last_writer.get(r)
            if w is not None:
                deps.add(w)
        for r in writes:
            w = self.last_writer.get(r)
            if w is not None:
                deps.add(w)
            for x in self.readers.get(r, ()):
                deps.add(x)
        for r in dict.fromkeys(reads):
            lst = self.readers.setdefault(r, [])
            if dma is None:
                lst[:] = [x for x in lst if not (self.ops[x]["dma"] is None and self.ops[x]["eng"] == eng)]
            lst.append(i)
        for r in writes:
            self.last_writer[r] = i
            self.readers[r] = []
        if dma is not None and dma not in self.dma_keys:
            self.dma_keys.append(dma)
        self.ops.append(dict(eng=eng, fn=fn, deps=deps, dma=dma))
        return i

    def emit(self, nc, stack, final_wait_eng="sp"):
        ops = self.ops
        n = len(ops)

        def skip(o, pj):
            return pj["dma"] is None and pj["eng"] == "pe" and o["eng"] == "pe" and o["dma"] is None

        signaling = [False] * n
        for o in ops:
            for j in o["deps"]:
                pj = ops[j]
                if pj["dma"] is None and not skip(o, pj):
                    signaling[j] = True
        esem = {e: stack.enter_context(nc.semaphore("s_" + e)) for e in ("pe", "act", "dve", "pool")}
        dsem = {k: stack.enter_context(nc.semaphore("d_%d" % idx)) for idx, k in enumerate(self.dma_keys)}
        cnt = {e: 0 for e in esem}
        dcnt = {k: 0 for k in dsem}
        token = [None] * n
        for i, o in enumerate(ops):
            if o["dma"] is not None:
                dcnt[o["dma"]] += 16
                token[i] = (("d", o["dma"]), dcnt[o["dma"]])
            elif signaling[i]:
                cnt[o["eng"]] += 1
                token[i] = (("e", o["eng"]), cnt[o["eng"]])
        per_eng = {e: [] for e in ENGS}
        for i, o in enumerate(ops):
            per_eng[o["eng"]].append(i)
        known = {e: {} for e in ENGS}
        known_at = [None] * n
        plan = [None] * n
        for i, o in enumerate(ops):
            E = o["eng"]
            kn = known[E]
            best = {}
            for j in sorted(o["deps"], reverse=True):
                pj = ops[j]
                if skip(o, pj):
                    continue
                sk, val = token[j]
                if kn.get(sk, 0) >= val:
                    continue
                best[sk] = max(best.get(sk, 0), val)
                kn[sk] = val
                ka = known_at[j]
                if ka:
                    for k2, v2 in ka.items():
                        if kn.get(k2, 0) < v2:
                            kn[k2] = v2
            plan[i] = [(sk, v) for sk, v in best.items()]
            ka = dict(kn)
            if token[i] is not None and o["dma"] is None:
                ka[token[i][0]] = token[i][1]
            known_at[i] = ka
        self.stats = dict(n_ops=n, n_signal=sum(signaling), n_waits=sum(len(p) for p in plan),
                          per_eng={e: len(v) for e, v in per_eng.items()}, n_dma_sems=len(dsem))

        def semof(sk):
            return esem[sk[1]] if sk[0] == "e" else dsem[sk[1]]

        block = stack.enter_context(nc.Block())
        engobj = {"pe": "tensor", "act": "scalar", "dve": "vector", "pool": "gpsimd", "sp": "sync"}

        def make(E):
            def body(eng):
                for i in per_eng[E]:
                    o = ops[i]
                    for sk, val in plan[i]:
                        eng.wait_ge(semof(sk), val)
                    inst = o["fn"](eng)
                    if token[i] is not None:
                        sk, val = token[i]
                        inst.then_inc(semof(sk), 16 if sk[0] == "d" else 1)
                if E == final_wait_eng:
                    for k in self.dma_keys:
                        if dcnt[k] > 0:
                            eng.wait_ge(dsem[k], dcnt[k])
            return body

        for E in ENGS:
            if per_eng[E] or E == final_wait_eng:
                getattr(block, engobj[E])(make(E))


class Rot:
    def __init__(self, name, tiles, keys=None):
        self.name, self.tiles, self.i = name, tiles, 0
        self.keys = keys if keys is not None else [(name, k) for k in range(len(tiles))]

    def get(self):
        k = self.i % len(self.tiles)
        self.i += 1
        return self.tiles[k], self.keys[k]


def build_program():
    nc = bass.Bass("TRN2", target_bir_lowering=False)
    st = ExitStack()
    S = Sched()

    def din(name, shape):
        return nc.dram_tensor(name, list(shape), F32, kind="ExternalInput").ap()

    def dout(name, shape):
        return nc.dram_tensor(name, list(shape), F32, kind="ExternalOutput").ap()

    x_all = din("x_all", [NT, D])
    w_ada = din("w_ada", [DEPTH, D, 3 * D])
    w_in = din("w_in", [DEPTH, D, 3328])
    w_mg = din("w_mg", [DEPTH, D, 4, D])
    w_bo = din("w_bo", [DEPTH, 4, 256, D])
    w_o = din("w_o", [DEPTH, D, D])
    small_d = din("small", [128, DEPTH, NS])
    cT_d = din("cT", [128, 8, 2])
    normf_d = din("normf", [128, 8])
    lruw_d = din("lruw", [DEPTH, 128, 8, 128])
    strips_d = din("strips", [DEPTH, 4, 128, NSTRIP])
    cstb_d = din("cstb", [128, NCB])
    cstf_d = din("cstf", [128, NCF])
    cdk_d = din("cdk", [DEPTH, 4, 2, PAST, 32])
    cdv_d = din("cdv", [DEPTH, 4, PAST, 64])
    cnk_d = din("cnk", [DEPTH, 4, PAST, 64])
    cnv_d = din("cnv", [DEPTH, 4, PAST, 64])
    csk_d = din("csk", [DEPTH, 2, PAST, 64])
    csv_d = din("csv", [DEPTH, 2, PAST, 64])

    y_all = dout("y_all", [NT, D])
    ndk = dout("ndk", [2, DEPTH, 4, 2, 256, 32])
    ndv = dout("ndv", [2, DEPTH, 4, 256, 64])
    nnk = dout("nnk", [2, DEPTH, 4, 256, 64])
    nnv = dout("nnv", [2, DEPTH, 4, 256, 64])
    nsk = dout("nsk", [2, DEPTH, 2, 256, 64])
    nsv = dout("nsv", [2, DEPTH, 2, 256, 64])
    nst = dout("nst", [2, DEPTH, 2, 256])

    wsc = nc.dram_tensor("wsc", [DEPTH, 17, 128, 8, 512], BF16, kind="Internal").ap()
    wsc_bo = nc.dram_tensor("wsc_bo", [DEPTH, 8, 128, 8, 128], BF16, kind="Internal").ap()

    def sb(name, shape, dt=F32):
        return st.enter_context(nc.sbuf_tensor(name, list(shape), dt))

    xT = sb("xT", [128, 8, NT])
    hT = sb("hT", [128, 8192], BF16)
    brg = sb("brg", [128, 8, 1024], BF16)
    qk = sb("qk", [128, 14, 1024], BF16)
    VA = sb("VA", [128, 8, 384], BF16)
    VC = sb("VC", [128, 8, 384], BF16)
    VD = sb("VD", [128, 8, 320], BF16)
    bxp = sb("bxp", [128, 2, 1040], BF16)
    xc = sb("xc", [128, 2, 1024], BF16)
    yrf = bxp
    wbufs = [sb("wb%d" % i, [128, 8, 512], BF16) for i in range(2)]
    wbos = [sb("wbo%d" % i, [128, 8, 128], BF16) for i in range(2)]
    cstb = sb("cstb_s", [128, NCB], BF16)
    cstf = sb("cstf_s", [128, NCF])
    stripb = [sb("strip%d" % i, [128, NSTRIP], BF16) for i in range(2)]
    lruw = sb("lruw_s", [128, 8, 128], BF16)
    cdiag = sb("cdiag", [128, 8, 128], BF16)
    small = sb("small_s", [128, DEPTH, NS])
    cT = sb("cT_s", [128, 8, 2])
    scT = sb("scT", [128, 8, 2], BF16)
    normf = sb("normf_s", [128, 8])
    modTs = [sb("modT%d" % i, [128, 24, 2]) for i in range(DEPTH)]
    sc1s = [sb("sc1_%d" % i, [128, 8, 2]) for i in range(DEPTH)]
    der = sb("der", [128, 32])
    hlast = sb("hlast", [128, 8])
    stout = sb("stout", [128, 8])
    Fp = Rot("F", [sb("F%d" % i, [128, 512]) for i in range(6)])
    Rp = Rot("R", [sb("R0", [128, 512])])
    Bp = Rot("B", [sb("B%d" % i, [128, 512], BF16) for i in range(2)])
    Bpt = Rot("BT", [sb("BT%d" % i, [128, 512], BF16) for i in range(8)])
    dummy = sb("dmy_t", [128, 4])
    _pst = [st.enter_context(nc.psum_tensor("ps%d" % i, [128, 512], F32)) for i in range(8)]
    PSp = Rot("PS", _pst)
    PSa = Rot("PS", _pst[0:4], keys=[("PS", k) for k in range(0, 4)])
    PSo = Rot("PS", _pst[4:8], keys=[("PS", k) for k in range(4, 8)])
    CK = [("kstg", 0), ("kstg", 1), ("kstg", 2), "cKA0", "cKA1", "cKC0", "cKC1", "cKD0", "cKD1"] + [("cVA", h) for h in range(5)] + [("cVC", h) for h in range(5)] + [("cVD", h) for h in range(3)]

    def fence(tag):
        S.op("pool", lambda e: e.memset(dummy[:, 0:1], 0.0), reads=[], writes=HT_ALL + CK + ["cfence"])

    HT_ALL = [("hT", i) for i in range(16)]

    def hkeys(kc, lo, hi):
        return [("hT", kc * 2 + b) for b in range(lo // 512, (hi - 1) // 512 + 1)]

    def hview(off, shape):
        n = int(np.prod(shape))
        v = hT[:, off:off + n]
        if len(shape) == 2:
            return v.rearrange("p (a b) -> p a b", a=shape[0])
        return v

    cKA = hview(0, [2, 512])
    cKC = hview(1024, [2, 512])
    cKD = hview(2048, [2, 512])
    cVA = hview(3072, [4, 384])
    cVC = hview(3072 + 1536, [4, 384])
    cVD = hview(3072 + 3072, [4, 320])

    def dma(eng, out, in_, reads, writes, key):
        S.op(eng, lambda e: e.dma_start(out=out, in_=in_), reads=reads, writes=writes, dma=key)

    def mm(out, lhsT, rhs, start, stop, reads, writes):
        if "cstb" in reads:
            reads = list(reads) + ["cstb2"]
        S.op("pe", lambda e: e.matmul(out, lhsT=lhsT, rhs=rhs, start=start, stop=stop), reads=reads, writes=writes)

    def act(out, in_, func, reads, writes, bias=None, scale=None):
        kw = {}
        if bias is not None:
            kw["bias"] = bias
        if scale is not None:
            kw["scale"] = scale
        S.op("act", lambda e: e.activation(out=out, in_=in_, func=func, **kw), reads=reads, writes=writes)

    def tt(out, in0, in1, op, reads, writes, eng="dve"):
        if "cstb" in reads:
            reads = list(reads) + ["cstb2"]
        S.op(eng, lambda e: e.tensor_tensor(out=out, in0=in0, in1=in1, op=op), reads=reads, writes=writes)

    def ts(out, in0, s1, op0, reads, writes, s2=None, op1=None, eng="dve"):
        if op1 is None:
            S.op(eng, lambda e: e.tensor_scalar(out=out, in0=in0, scalar1=s1, scalar2=None, op0=op0), reads=reads, writes=writes)
        else:
            S.op(eng, lambda e: e.tensor_scalar(out=out, in0=in0, scalar1=s1, scalar2=s2, op0=op0, op1=op1), reads=reads, writes=writes)

    def stt(out, in0, scalar, in1, op0, op1, reads, writes):
        S.op("dve", lambda e: e.scalar_tensor_tensor(out=out, in0=in0, scalar=scalar, in1=in1, op0=op0, op1=op1),
             reads=reads, writes=writes)

    def cp(out, in_, reads, writes, eng="dve"):
        S.op(eng, lambda e: e.tensor_copy(out=out, in_=in_), reads=reads, writes=writes)

    def recip(out, in_, reads, writes):
        S.op("dve", lambda e: e.reciprocal(out=out, in_=in_), reads=reads, writes=writes)

    def memset(ap, val, writes, eng="dve"):
        S.op(eng, lambda e: e.memset(ap, val), writes=writes)

    ident_b = cstb[:, C_ID:C_ID + 128]
    i8_b = cstb[:, C_I8:C_I8 + 128]
    ones_b = cstb[:, C_ONES:C_ONES + 128]
    bones_b = cstb[:, C_BONES:C_BONES + 128]
    ident_f = cstf[:, 0:128]
    eps_c = cstf[:, 130:131]

    dma("pool", cstb[:, 0:768], cstb_d[:, 0:768], [], ["cstb"], "su0")
    dma("sp", cstf[:], cstf_d, [], ["cstf"], "su1")
    dma("sp", small[:], small_d, [], ["small"], "su2")
    dma("sp", cT[:], cT_d, [], ["cT"], "su3")
    dma("sp", normf[:], normf_d, [], ["normf"], "su4")
    for (Vt, name, nt, cols) in ((VA, "VA", 8, (64, 256)), (VC, "VC", 8, (64, 256)), (VD, "VD", 8, (0, 128, 256))):
        for c0 in cols:
            memset(Vt[:, :, c0:c0 + 64], 1.0, [(name, t) for t in range(nt)], eng="pool")
    memset(bxp[:], 0.0, ["bxp"], eng="pool")
    act(scT[:], cT[:], AF.Silu, ["cT"], ["scT"])

    wb_rot = Rot("wb", wbufs)
    wbo_rot = Rot("wbo", wbos)

    def load_w(src_ap, ncols=512, four=False, cache=None):
        wt, wk = wb_rot.get()
        allk = [wk] + [(wk, n) for n in range(4)]
        if cache is not None and not cache[2]:
            l_, idx, _ = cache
            dma("pool", wt[:, :, 0:ncols], wsc[l_, idx, :, :, 0:ncols], [("wsc", l_, idx)], allk, wk)
            return wt, allk
        if four:
            for n in range(4):
                dma("pool", wt[:, :, n * 128:(n + 1) * 128], src_ap(n), [], [(wk, n)] + ([wk] if n == 0 else []), wk)
        elif ncols == 512:
            dma("pool", wt[:], src_ap, [], allk, wk)
        else:
            dma("pool", wt[:, :, 0:ncols], src_ap, [], allk, wk)
        if cache is not None and cache[2]:
            l_, idx, _ = cache
            dma("sp", wsc[l_, idx, :, :, 0:ncols], wt[:, :, 0:ncols], allk, [("wsc", l_, idx)], ("wst", wk[1]))
        return wt, allk

    def rms_rstd(tok0, blk3, lnexp=False):
        ps, pk = PSp.get()
        for kc in range(8):
            sq, sqk = Bp.get()
            act(sq[:], xT[:, kc, tok0:tok0 + 512], AF.Square, [("xT", kc, blk3)], [sqk])
            mm(ps[:], ones_b, sq[:], kc == 0, kc == 7, [sqk, "cstb"], [pk])
        r, rk = Rp.get()
        if lnexp:
            act(r[:], ps[:], AF.Ln, [pk, "cstf"], [rk], bias=eps_c, scale=1.0 / D)
            act(r[:], r[:], AF.Exp, [rk], [rk], scale=-0.5)
        else:
            act(r[:], ps[:], AF.Sqrt, [pk, "cstf"], [rk], bias=eps_c, scale=1.0 / D)
            recip(r[:], r[:], [rk], [rk])
        return r, rk

    def emit_h(P, b, l, lnexp=False):
        modT, sc1 = modTs[l], sc1s[l]
        tok0 = P["t0"] + 512 * b
        blk3 = tok0 // 512
        g = P["g"]
        r, rk = rms_rstd(tok0, blk3, lnexp)
        for kc in range(8):
            tmp, tk = Fp.get()
            tt(tmp[:], xT[:, kc, tok0:tok0 + 512], r[:], ALU.mult, [("xT", kc, blk3), rk], [tk])
            act(hT[:, kc * 1024 + 512 * b: kc * 1024 + 512 * b + 512], tmp[:], AF.Identity, [tk, ("sc1", l), ("modT", l)], [("hT", kc * 2 + b)],
                bias=modT[:, kc, g:g + 1], scale=sc1[:, kc, g:g + 1])

    def hsl(kc, lo, n):
        return hT[:, kc * 1024 + lo: kc * 1024 + lo + n]

    def fm_proj(wt, wk, col0, P, b):
        ps, pk = PSp.get()
        for kc in range(8):
            mm(ps[:], wt[:, kc, col0:col0 + 128], hsl(kc, 512 * b, 512), kc == 0, kc == 7, wk + [("hT", kc * 2 + b)], [pk])
        return ps, pk

    def tm_proj(wt, wk, col0, ncols, tile):
        ps, pk = PSp.get()
        for kc in range(8):
            mm(ps[:, 0:ncols], hsl(kc, tile * 128, 128), wt[:, kc, col0:col0 + ncols], kc == 0, kc == 7,
               wk + [("hT", kc * 2 + tile // 4)], [pk])
        return ps, pk

    def mod_gen(l, pool, blocks=(0, 1, 2, 3, 4, 5)):
        sm = small[:, l, :]
        modT, sc1 = modTs[l], sc1s[l]
        pend = None
        for step in range(len(blocks) + 1):
            cur = None
            j = blocks[step] if step < len(blocks) else None
            if j is not None:
                cur = load_w(w_ada[l][:, j * 512:(j + 1) * 512].rearrange("(kc p) n -> p kc n", p=128))
            if pend is not None:
                (wt, wk), jj = pend
                ps, pk = pool.get()
                for o4 in range(4):
                    for kc in range(8):
                        mm(ps[:, o4 * 2:o4 * 2 + 2], wt[:, kc, o4 * 128:(o4 + 1) * 128], scT[:, kc, :], kc == 0, kc == 7, wk + ["scT"], [pk])
                tt(modT[:, 4 * jj:4 * jj + 4, :].rearrange("p a b -> p (a b)"), ps[:, 0:8], sm[:, 16 + 8 * jj:24 + 8 * jj], ALU.add,
                   [pk, "small"], [("modT", l)])
            pend = (cur, j) if cur is not None else None
            yield
        if 3 in blocks:
            stt(sc1[:].rearrange("p a b -> p (a b)"), modT[:, 8:16, :].rearrange("p a b -> p (a b)"), 1.0, sm[:, 0:16],
                ALU.add, ALU.mult, [("modT", l), "small"], [("sc1", l)])
        yield

    def layer_setup(l):
        sm = small[:, l, :]
        act(der[:, 0:4], sm[:, 114:118], AF.Exp, ["small"], ["der_a"], scale=-1.0)
        act(der[:, 4:8], der[:, 0:4], AF.Ln, ["der_a"], ["der_b"], bias=1.0)
        ts(der[:, 8:12], der[:, 4:8], -8.0, ALU.mult, ["der_b"], ["der_ca"])
        ts(der[:, 24:28], sm[:, 106:110], -1.0, ALU.mult, ["small"], ["der_nb"])
        ts(der[:, 28:32], sm[:, 110:114], -1.0, ALU.mult, ["small"], ["der_nb"])
        act(der[:, 12:16], sm[:, 119:123], AF.Exp, ["small"], ["der_es"])
        lam_init = 0.8 - 0.6 * math.exp(-0.3 * l)
        t1, t1k = Fp.get()
        tt(t1[:, 0:32], sm[:, 128:160], sm[:, 160:192], ALU.mult, ["small"], [t1k])
        tt(t1[:, 32:64], sm[:, 192:224], sm[:, 224:256], ALU.mult, ["small"], [t1k])
        S.op("dve", lambda e: e.tensor_reduce(out=der[:, 16:17], in_=t1[:, 0:32], axis=AX.X, op=ALU.add), reads=[t1k], writes=["der_l1"])
        S.op("dve", lambda e: e.tensor_reduce(out=der[:, 17:18], in_=t1[:, 32:64], axis=AX.X, op=ALU.add), reads=[t1k], writes=["der_l2"])
        act(der[:, 18:20], der[:, 16:18], AF.Exp, ["der_l1", "der_l2"], ["der_l3"])
        tt(der[:, 20:21], der[:, 19:20], der[:, 18:19], ALU.subtract, ["der_l3"], ["der_l4"])
        ts(der[:, 21:22], der[:, 20:21], -lam_init, ALU.add, ["der_l4"], ["der_nl"])
        ts(der[:, 22:23], sm[:, 118:119], 1.0 - lam_init, ALU.mult, ["small"], ["der_dg"])
        for c in range(2):
            for j in range(4):
                ts(cdiag[:, c * 4 + j, :], ident_b, sm[:, 96 + c * 4 + j:97 + c * 4 + j], ALU.mult, ["cstb", "small"], ["cdiag"])
        dma("pool", lruw[:], lruw_d[l], [], ["lruw"], "lruw")

    def qk_evac(P, ps, pk, dst_chunk, b, kind, l):
        lat = P["lat"]
        tl = 512 * b
        if kind == "plain" or (not lat and kind in ("ropeA", "ropeD")):
            act(qk[:, dst_chunk, tl:tl + 512], ps[:], AF.Copy, [pk], [("qk", dst_chunk, b)])
            return
        if not lat and kind == "aq":
            for m in range(2):
                ts(qk[:, dst_chunk + m, tl:tl + 512], ps[:], cstf[:, 128 + m:129 + m], ALU.mult, [pk, "cstf"], [("qk", dst_chunk + m, b)])
            return
        isA = kind in ("ropeA", "aq")
        Rm = cstb[:, C_RA:C_RA + 128] if isA else cstb[:, C_RD:C_RD + 128]
        cosT = cstb[:, (C_COSA if isA else C_COSD) + tl:(C_COSA if isA else C_COSD) + tl + 512]
        sinT = cstb[:, (C_SINA if isA else C_SIND) + tl:(C_SINA if isA else C_SIND) + tl + 512]
        yb, ybk = Bp.get()
        act(yb[:], ps[:], AF.Copy, [pk], [ybk])
        ps2, pk2 = PSp.get()
        mm(ps2[:], Rm, yb[:], True, True, [ybk, "cstb"], [pk2])
        t1, t1k = Fp.get()
        tt(t1[:], ps[:], cosT, ALU.mult, [pk, "cstb"], [t1k])
        t2, t2k = Fp.get()
        tt(t2[:], ps2[:], sinT, ALU.mult, [pk2, "cstb"], [t2k])
        if kind == "aq":
            tt(t1[:], t1[:], t2[:], ALU.add, [t1k, t2k], [t1k])
            for m in range(2):
                ts(qk[:, dst_chunk + m, tl:tl + 512], t1[:], cstf[:, 128 + m:129 + m], ALU.mult, [t1k, "cstf"], [("qk", dst_chunk + m, b)])
        else:
            tt(qk[:, dst_chunk, tl:tl + 512], t1[:], t2[:], ALU.add, [t1k, t2k], [("qk", dst_chunk, b)])

    def v_evac(ps, pk, Vt, name, tile, kindD):
        if not kindD:
            cp(Vt[:, tile, 0:64], ps[:, 0:64], [pk], [(name, tile)])
            act(Vt[:, tile, 128:256], ps[:, 64:192], AF.Copy, [pk], [(name, tile)])
            cp(Vt[:, tile, 320:384], ps[:, 192:256], [pk], [(name, tile)])
        else:
            cp(Vt[:, tile, 64:128], ps[:, 0:64], [pk], [(name, tile)])
            act(Vt[:, tile, 192:256], ps[:, 64:128], AF.Copy, [pk], [(name, tile)])

    def kv_out(P, ps, pk, ncols, tile, dst_ap_fn, l, dd=64):
        stg, sk = Fp.get()
        act(stg[:, 0:ncols], ps[:, 0:ncols], AF.Copy, [pk], [sk])
        bl = tile // 2
        s0 = (tile % 2) * 128
        dma("sp", dst_ap_fn(bl, s0), stg[:, 0:ncols].rearrange("p (a d) -> p a d", d=dd), [sk], [], ("Fd", sk[1]))

    def projections(P, l):
        gen = [None]
        tk = [0]

        def tick():
            tk[0] += 1
            if gen[0] is not None and tk[0] % 9 == 0:
                next(gen[0], None)

        lat = P["lat"]
        nb = P["T"] // 512
        ntile = P["T"] // 128
        wl = w_in[l]

        def blk(c0, n=512):
            return wl[:, c0:c0 + n].rearrange("(kc p) n -> p kc n", p=128)

        if lat:
            memset(bxp[:, :, 0:1], 0.0, ["bxp"])
            memset(bxp[:, :, 1025:1028], 0.0, ["bxp"])
        else:
            memset(bxp[:, :, 0:1], 0.0, ["bxp"])
            memset(bxp[:, :, 257:261], 0.0, ["bxp"])
            memset(bxp[:, :, 517:520], 0.0, ["bxp"])
        for j in range(2):
            wt, wk = load_w(blk(j * 512), cache=(l, j, lat))
            if j == 1 and not cst2_loaded:
                dma("pool", cstb[:, 768:NCB], cstb_d[:, 768:NCB], [], ["cstb2"], "su5")
                cst2_loaded.append(1)
            for o4 in range(4):
                for b in range(nb):
                    ps, pk = fm_proj(wt, wk, o4 * 128, P, b)
                    act(brg[:, j * 4 + o4, 512 * b:512 * b + 512], ps[:], AF.Silu, [pk], [("brg", j * 4 + o4, b)])
        wt, wk = load_w(blk(1536), cache=(l, 3, lat))
        for tile in range(ntile):
            ps, pk = tm_proj(wt, wk, 0, 256, tile)
            v_evac(ps, pk, VA, "VA", tile, False)
            if not lat:
                kv_out(P, ps, pk, 256, tile, lambda bl, s0: ndv[bl, l, :, s0:s0 + 128, :].rearrange("h s d -> s h d"), l)
        for c in range(2):
            for b in range(nb):
                ps, pk = fm_proj(wt, wk, 256 + c * 128, P, b)
                if lat:
                    cp(bxp[:, c, 1 + 512 * b:1 + 512 * b + 512], ps[:], [pk], ["bxp"])
                else:
                    cp(bxp[:, c, 1:257], ps[:, 0:256], [pk], ["bxp"])
                    cp(bxp[:, c, 261:517], ps[:, 256:512], [pk], ["bxp"])
        if not lat:
            gen[0] = lru(P, l)
        wt, wk = load_w(blk(1024), cache=(l, 2, lat))
        for c in range(2):
            for b in range(nb):
                ps, pk = fm_proj(wt, wk, c * 128, P, b)
                qk_evac(P, ps, pk, 2 * c, b, "aq", l)
                tick()
        for c in range(2):
            for b in range(nb):
                ps, pk = fm_proj(wt, wk, 256 + c * 128, P, b)
                qk_evac(P, ps, pk, 4 + c, b, "ropeA", l)
                tick()
        if not lat:
            for tile in range(ntile):
                ps, pk = tm_proj(wt, wk, 256, 256, tile)
                kv_out(P, ps, pk, 256, tile,
                       lambda bl, s0: ndk[bl, l, :, :, s0:s0 + 128, :].rearrange("h m s d -> s (h m) d"), l, dd=32)
        wt, wk = load_w(blk(2048), cache=(l, 4, lat))
        for c in range(4):
            for b in range(nb):
                ps, pk = fm_proj(wt, wk, c * 128, P, b)
                qk_evac(P, ps, pk, 6 + c, b, "plain", l)
                tick()
        if not lat:
            for tile in range(ntile):
                ps, pk = tm_proj(wt, wk, 256, 256, tile)
                kv_out(P, ps, pk, 256, tile, lambda bl, s0: nnk[bl, l, :, s0:s0 + 128, :].rearrange("h s d -> s h d"), l)
        wt, wk = load_w(blk(2560), cache=(l, 5, lat))
        for tile in range(ntile):
            ps, pk = tm_proj(wt, wk, 0, 256, tile)
            v_evac(ps, pk, VC, "VC", tile, False)
            tick()
            if not lat:
                kv_out(P, ps, pk, 256, tile, lambda bl, s0: nnv[bl, l, :, s0:s0 + 128, :].rearrange("h s d -> s h d"), l)
        for c in range(2):
            for b in range(nb):
                ps, pk = fm_proj(wt, wk, 256 + c * 128, P, b)
                qk_evac(P, ps, pk, 10 + c, b, "ropeD", l)
                tick()
        wt, wk = load_w(blk(3072, 256), 256, cache=(l, 6, lat))
        for b in range(nb):
            ps, pk = fm_proj(wt, wk, 0, P, b)
            qk_evac(P, ps, pk, 12, b, "ropeD", l)
            tl = 512 * b
            act(qk[64:128, 13, tl:tl + 512], qk[0:64, 12, tl:tl + 512], AF.Copy, [("qk", 12, b)], [("qk", 13, b)])
            act(qk[0:64, 13, tl:tl + 512], qk[64:128, 12, tl:tl + 512], AF.Copy, [("qk", 12, b)], [("qk", 13, b)])
        for tile in range(ntile):
            ps, pk = tm_proj(wt, wk, 128, 128, tile)
            v_evac(ps, pk, VD, "VD", tile, True)
            tick()
            if not lat:
                kv_out(P, ps, pk, 128, tile, lambda bl, s0: nsv[bl, l, :, s0:s0 + 128, :].rearrange("h s d -> s h d"), l)
        if not lat:
            for tile in range(ntile):
                ps, pk = tm_proj(wt, wk, 0, 128, tile)
                kv_out(P, ps, pk, 128, tile, lambda bl, s0: nsk[bl, l, :, s0:s0 + 128, :].rearrange("h s d -> s h d"), l)
        if gen[0] is not None:
            for _ in gen[0]:
                pass

    kst = hT[:, 7424:8192].bitcast(F32)

    def load_cache(l):
        fence("pre")
        voff = (0, 128, 192, 320)

        def srcs_A(c, j):
            return [(0, cdk_d[l, 2 * c:2 * c + 2, :, j * 128:(j + 1) * 128, :].rearrange("h m k d -> k (h m) d"), 128, 32)]

        def srcs_C(c, j):
            return [(0, cnk_d[l, 2 * c:2 * c + 2, j * 128:(j + 1) * 128, :].rearrange("h k d -> k h d"), 128, 64)]

        def srcs_D(c, j):
            if c == 0:
                return [(0, csk_d[l, :, j * 128:(j + 1) * 128, :].rearrange("h k d -> k h d"), 128, 64)]
            return [(0, csk_d[l, 1, j * 128:(j + 1) * 128, :], 64, None), (64, csk_d[l, 0, j * 128:(j + 1) * 128, :], 64, None)]

        for c in range(2):
            ps, pk = PSp.get()
            for j in range(4):
                stg, sk = Fp.get()
                for (dcol, src, n, dd) in srcs_A(c, j):
                    dv = stg[:, dcol:dcol + n]
                    if dd is not None:
                        dv = dv.rearrange("p (a d) -> p a d", d=dd)
                    dma("sp", dv, src, [], [sk], ("Fd", sk[1]))
                S.op("pe", lambda e, ps=ps, stg=stg, j=j: e.transpose(ps[:, j * 128:(j + 1) * 128], stg[:, 0:128], ident_f),
                     reads=[sk, "cstf"], writes=[pk])
            cp(cKA[:, c, :], ps[:], [pk, "cfence"], ["cKA" + str(c)])
        for h in range(4):
            dma("pool", cVA[:, :, voff[h]:voff[h] + 64], cdv_d[l, h].rearrange("(j p) d -> p j d", p=128), ["cfence"], [("cVA", h)], ("cva", h))
        for c0 in (64, 256):
            S.op("pool", lambda e, c0=c0: e.memset(cVA[:, :, c0:c0 + 64], 1.0), reads=["cfence"], writes=[("cVA", 4)])

        def rest():
            for h in range(4):
                dma("pool", cVC[:, :, voff[h]:voff[h] + 64], cnv_d[l, h].rearrange("(j p) d -> p j d", p=128), ["cfence"], [("cVC", h)], ("cvc", h))
            for kv in range(2):
                dma("pool", cVD[:, :, 64 + 128 * kv:128 + 128 * kv], csv_d[l, kv].rearrange("(j p) d -> p j d", p=128), ["cfence"], [("cVD", kv)], ("cvd", kv))
            for (v, name, cols, kk) in ((cVC, "cVC", (64, 256), 4), (cVD, "cVD", (0, 128, 256), 2)):
                for c0 in cols:
                    S.op("pool", lambda e, v=v, c0=c0: e.memset(v[:, :, c0:c0 + 64], 1.0), reads=["cfence"], writes=[(name, kk)])
            tiles = [(cKC, "cKC", c, j, srcs_C(c, j)) for c in range(2) for j in range(4)] + \
                    [(cKD, "cKD", c, j, srcs_D(c, j)) for c in range(2) for j in range(4)]

            def issue(n):
                (_, _, _, _, srcs) = tiles[n]
                si = n % 3
                for (dcol, src, nn, dd) in srcs:
                    dv = kst[:, si * 128 + dcol:si * 128 + dcol + nn]
                    if dd is not None:
                        dv = dv.rearrange("p (a d) -> p a d", d=dd)
                    dma("sp", dv, src, ["cfence"], [("kstg", si)], ("kd", si))

            issue(0)
            issue(1)
            yield
            for n in range(len(tiles)):
                (dst, name, c, j, _) = tiles[n]
                si = n % 3
                ps, pk = PSa.get()
                S.op("pe", lambda e, ps=ps, si=si: e.transpose(ps[:, 0:128], kst[:, si * 128:si * 128 + 128], ident_f),
                     reads=[("kstg", si), "cstf"], writes=[pk])
                cp(dst[:, c, j * 128:(j + 1) * 128], ps[:, 0:128], [pk, "cfence"], [name + str(c)])
                if n + 2 < len(tiles):
                    issue(n + 2)
                yield

        return rest()

    def lru(P, l, scr=None, pool=None, fine=False):
        lat = P["lat"]
        sm = small[:, l, :]
        nb = P["T"] // 512
        pool = pool or PSp
        for c in range(2):
            for b in range(nb):
                ps, pk = pool.get()
                if lat:
                    for j in range(4):
                        mm(ps[:], cdiag[:, c * 4 + j, :], bxp[:, c, 512 * b + j:512 * b + j + 512], j == 0, j == 3, ["cdiag", "bxp"], [pk])
                else:
                    for s_ in range(2):
                        for j in range(4):
                            mm(ps[:, 256 * s_:256 * s_ + 256], cdiag[:, c * 4 + j, :], bxp[:, c, 260 * s_ + j:260 * s_ + j + 256],
                               j == 0, j == 3, ["cdiag", "bxp"], [pk])
                act(xc[:, c, 512 * b:512 * b + 512], ps[:], AF.Identity, [pk, "small"], [("xc", c, b)], bias=sm[:, 104 + c:105 + c])
                yield
        segs = [(0, 512)] if lat else [(0, 256), (256, 256)]
        for d in range(2):
            order = list(range(nb)) if d == 0 else list(range(nb - 1, -1, -1))
            for bi, b in enumerate(order):
                CH = []
                for c in range(2):
                    xcs = xc[:, c, 512 * b:512 * b + 512]
                    psr, pkr = pool.get()
                    mm(psr[:], lruw[:, d * 4 + 0 + c, :], xcs, True, True, ["lruw", ("xc", c, b)], [pkr])
                    psi, pki = pool.get()
                    mm(psi[:], lruw[:, d * 4 + 2 + c, :], xcs, True, True, ["lruw", ("xc", c, b)], [pki])
                    if scr is None:
                        t3 = []
                        for _ in range(3):
                            t_, k_ = Fp.get()
                            t3.append((t_[:], [k_]))
                    else:
                        t3 = scr[3 * c:3 * c + 3]
                    (r, rk), (ii, ik), (a, ak) = t3
                    CH.append(dict(c=c, xcs=xcs, psr=psr, pkr=pkr, psi=psi, pki=pki, r=r, rk=rk, ii=ii, ik=ik, a=a, ak=ak,
                                   hcol=hlast[:, d * 2 + c:d * 2 + c + 1]))
                for q in CH:
                    c = q["c"]
                    act(q["r"], q["psr"][:], AF.Sigmoid, [q["pkr"], "small"], q["rk"], bias=sm[:, 106 + d * 2 + c:107 + d * 2 + c])
                    act(q["ii"], q["psi"][:], AF.Sigmoid, [q["pki"], "small"], q["ik"], bias=sm[:, 110 + d * 2 + c:111 + d * 2 + c])
                if fine:
                    yield
                for q in CH:
                    c = q["c"]
                    act(q["a"], q["r"], AF.Exp, q["rk"] + ["der_ca"], q["ak"], scale=der[:, 8 + d * 2 + c:9 + d * 2 + c])
                if fine:
                    yield
                for q in CH:
                    tt(q["r"], q["a"], q["a"], ALU.mult, q["ak"], q["rk"])
                    tt(q["ii"], q["ii"], q["xcs"], ALU.mult, q["ik"] + [("xc", q["c"], b)], q["ik"])
                if fine:
                    yield
                for q in CH:
                    act(q["r"], q["r"], AF.Ln, q["rk"], q["rk"], bias=1.0, scale=-1.0)
                for q in CH:
                    act(q["r"], q["r"], AF.Exp, q["rk"], q["rk"], scale=0.5)
                if fine:
                    yield
                for q in CH:
                    tt(q["ii"], q["ii"], q["r"], ALU.mult, q["ik"] + q["rk"], q["ik"])
                if fine:
                    yield
                for q in CH:
                    c = q["c"]
                    a, ak, ii, ik, hcol = q["a"], q["ak"], q["ii"], q["ik"], q["hcol"]
                    hs, hk = q["r"], q["rk"]
                    for (s0, sn) in segs:
                        if lat:
                            init = sm[:, 123 + d * 2 + c:124 + d * 2 + c] if bi == 0 else hcol
                            ireads = ["small"] if bi == 0 else [("hlast", d, c)]
                        else:
                            init = 0.0
                            ireads = []
                        if d == 0:
                            o_, a_, u_ = hs[:, s0:s0 + sn], a[:, s0:s0 + sn], ii[:, s0:s0 + sn]
                        else:
                            o_, a_, u_ = hs[:, s0:s0 + sn][:, ::-1], a[:, s0:s0 + sn][:, ::-1], ii[:, s0:s0 + sn][:, ::-1]
                        S.op("dve", lambda e, o_=o_, a_=a_, u_=u_, init=init: e.tensor_tensor_scan(
                            out=o_, data0=a_, data1=u_, initial=init, op0=ALU.mult, op1=ALU.add),
                            reads=ak + ik + ireads, writes=hk)
                        if lat:
                            if bi < nb - 1:
                                lastc = s0 + sn - 1 if d == 0 else s0
                                cp(hcol, hs[:, lastc:lastc + 1], hk, [("hlast", d, c)])
                        else:
                            lastc = s0 + sn - 1 if d == 0 else s0
                            sidx = s0 // 256
                            cp(stout[:, sidx * 4 + d * 2 + c:sidx * 4 + d * 2 + c + 1], hs[:, lastc:lastc + 1], hk, ["stout"])
                if fine:
                    yield
                for q in CH:
                    c = q["c"]
                    hs, hk = q["r"], q["rk"]
                    if d == 0:
                        cp(yrf[:, c, 512 * b:512 * b + 512], hs, hk, ["bxp"])
                    else:
                        tt(hs, hs, yrf[:, c, 512 * b:512 * b + 512], ALU.add, hk + ["bxp"], hk)
                        tt(brg[:, 2 + c, 512 * b:512 * b + 512], hs, brg[:, 2 + c, 512 * b:512 * b + 512], ALU.mult,
                           hk + [("brg", 2 + c, b)], [("brg", 2 + c, b)])
                yield
        if not lat:
            for bb in range(2):
                dma("sp", nst[bb, l, :, :].rearrange("d (c p) -> p d c", p=128),
                    stout[:, bb * 4:bb * 4 + 4].rearrange("p (d c) -> p d c", d=2), ["stout"], [], "ost%d" % bb)

    GRP = 4

    strip_pref = {}
    carry = []

    def run_jobs(jobs, hook=None, hook_every=6, keep_tail=False):
        items = []
        for a in range(0, len(jobs), 2):
            ja, jb = jobs[a], jobs[a + 1]
            assert len(ja["keys"]) == len(jb["keys"])
            for i in range(len(ja["keys"])):
                items.append((a, i))
                items.append((a + 1, i))
        groups = [items[a:a + GRP] for a in range(0, len(items), GRP)]
        acc = {}
        prev = None
        deferred = [(3, f) for f in carry]
        del carry[:]
        for g in range(len(groups) + 1):
            cur = None
            for dfn in [f for (ga, f) in deferred if ga <= g]:
                dfn()
            deferred = [(ga, f) for (ga, f) in deferred if ga > g]
            if hook is not None and g % hook_every == (1 if hook_every > 1 else 0):
                hook()
            if g < len(groups):
                cur = []
                tiles = []
                for (j, i) in groups[g]:
                    job = jobs[j]
                    if i == 0 and job.get("pre") is not None:
                        job["pre"]()
                    kd = job["keys"][i]
                    Nq = job["Nq"]
                    pss, pks = PSa.get()
                    ex = kd.get("extra", [])
                    mm(pss[:, 0:Nq], kd["kT"], job["q_ap"], True, len(ex) == 0, kd["kreads"] + job["qreads"], [pks])
                    tiles.append((pss, pks, ex, Nq, job))
                for xi in range(2):
                    for (pss, pks, ex, Nq, job) in tiles:
                        if xi < len(ex):
                            xl, xr, xreads = ex[xi]
                            mm(pss[:, 0:Nq], xl, xr, False, xi == len(ex) - 1, xreads, [pks])
                for n_, (j, i) in enumerate(groups[g]):
                    (pss, pks, ex, Nq, job) = tiles[n_]
                    pt, ptk = Bpt.get()
                    act(pt[:, 0:Nq], pss[:, 0:Nq], AF.Exp, [pks], [ptk], scale=job["scale"])
                    cur.append((j, i, pt, ptk))
            if prev is not None:
                allk = [ptk for (_, _, _, ptk) in prev]
                for n_, (j, i, pt, ptk) in enumerate(prev):
                    job = jobs[j]
                    kd = job["keys"][i]
                    Nq = job["Nq"]
                    if i == 0:
                        acc[j] = PSo.get()
                    pso, pko = acc[j]
                    nk = len(job["keys"])
                    mm(pso[:, 0:Nq], kd["v"], pt[:, 0:Nq], i == 0, i == nk - 1, kd["vreads"] + (allk if n_ == 0 else [ptk]), [pko])
                    if i == nk - 1:
                        later = job["epi"](pso, pko)
                        if later is not None:
                            deferred.append((g + 5, later))
                        del acc[j]
            prev = cur
        if keep_tail:
            carry.extend(f for (ga, f) in deferred)
        else:
            for (ga, f) in deferred:
                f()

    def norm_out(pso, pko, Nq, odd, add_es=None, on_act=False):
        o_lo, d_lo = (64, 0) if odd else (0, 64)
        rc, rck = Fp.get()
        if on_act:
            if add_es is not None:
                act(rc[o_lo:o_lo + 64, 0:Nq], pso[d_lo:d_lo + 64, 0:Nq], AF.Ln, [pko, "der_es"], [rck], bias=der[d_lo:d_lo + 64, add_es:add_es + 1])
            else:
                act(rc[o_lo:o_lo + 64, 0:Nq], pso[d_lo:d_lo + 64, 0:Nq], AF.Ln, [pko], [rck])
            act(rc[o_lo:o_lo + 64, 0:Nq], rc[o_lo:o_lo + 64, 0:Nq], AF.Exp, [rck], [rck], scale=-1.0)
        elif add_es is not None:
            ts(rc[o_lo:o_lo + 64, 0:Nq], pso[d_lo:d_lo + 64, 0:Nq], der[d_lo:d_lo + 64, add_es:add_es + 1], ALU.add, [pko, "der_es"], [rck])
            recip(rc[o_lo:o_lo + 64, 0:Nq], rc[o_lo:o_lo + 64, 0:Nq], [rck], [rck])
        else:
            recip(rc[o_lo:o_lo + 64, 0:Nq], pso[d_lo:d_lo + 64, 0:Nq], [pko], [rck])
        tt(rc[o_lo:o_lo + 64, 0:Nq], pso[o_lo:o_lo + 64, 0:Nq], rc[o_lo:o_lo + 64, 0:Nq], ALU.mult, [pko, rck], [rck])
        return rc, rck, o_lo

    def q_blocks(P):
        if P["lat"]:
            return [(0, 512, 0, 0), (512, 512, 1, 0)]
        return [(0, 256, 0, 0), (256, 256, 0, 1)]

    def new_keys(P, seq):
        if P["lat"]:
            return [(128 * j, j) for j in range(8)]
        return [(256 * seq + 128 * j, 2 * seq + j) for j in range(2)]

    offA = (0, 64, 192, 256)

    def attn_A(P, l, hook=None, hook_every=6):
        lat = P["lat"]
        scale = 32.0 ** -0.5
        jobs = []
        for c in range(2):
            for (tl, Nq, bidx, seq) in q_blocks(P):
                grp = {}

                def finish_group(grp=grp, c=c, tl=tl, Nq=Nq, bidx=bidx):
                    dch, dk_ = grp["dch"]
                    sq, sqk = Bp.get()
                    tt(sq[:, 0:Nq], dch[:, 0:Nq], dch[:, 0:Nq], ALU.mult, [dk_], [sqk])
                    psn, pkn = PSa.get()
                    mm(psn[:, 0:Nq], bones_b, sq[:, 0:Nq], True, True, [sqk, "cstb"], [pkn])
                    rs, rsk = Fp.get()
                    act(rs[:, 0:Nq], psn[:, 0:Nq], AF.Ln, [pkn, "cstf"], [rsk], bias=eps_c, scale=1.0 / 64)
                    act(rs[:, 0:Nq], rs[:, 0:Nq], AF.Exp, [rsk], [rsk], scale=-0.5)
                    stt(dch[:, 0:Nq], dch[:, 0:Nq], der[:, 22:23], rs[:, 0:Nq], ALU.mult, ALU.mult, [dk_, rsk, "der_dg"], [dk_])
                    tt(brg[:, c, tl:tl + Nq], dch[:, 0:Nq], brg[:, c, tl:tl + Nq], ALU.mult, [dk_, ("brg", c, bidx)], [("brg", c, bidx)])

                for m in range(2):
                    for hh in range(2):
                        h = 2 * c + hh
                        rows = slice(64 * hh, 64 * hh + 64)
                        kl = []
                        for (ko, tile) in new_keys(P, seq):
                            kl.append(dict(kT=qk[rows, 4 + c, ko:ko + 128], kreads=[("qk", 4 + c, ko // 512)],
                                           v=VA[:, tile, offA[h]:offA[h] + 128], vreads=[("VA", tile)]))
                        if lat:
                            for j in range(4):
                                kl.append(dict(kT=cKA[rows, c, 128 * j:128 * j + 128], kreads=["cKA%d" % c],
                                               v=cVA[:, j, offA[h]:offA[h] + 128], vreads=[("cVA", x) for x in range(5)]))

                        def epi(pso, pko, grp=grp, hh=hh, m=m, Nq=Nq, fin=finish_group):
                            if "dch" not in grp:
                                grp["dch"] = Fp.get()
                            dch, dk_ = grp["dch"]
                            r = norm_out(pso, pko, Nq, hh == 1, on_act=True)
                            if m == 0:
                                grp["r0", hh] = r
                            else:
                                (r0, r0k, o_lo) = grp["r0", hh]
                                (r1, r1k, _) = r
                                stt(dch[o_lo:o_lo + 64, 0:Nq], r1[o_lo:o_lo + 64, 0:Nq], der[o_lo:o_lo + 64, 21:22], r0[o_lo:o_lo + 64, 0:Nq],
                                    ALU.mult, ALU.add, [r0k, r1k, "der_nl"], [dk_])
                                if hh == 1:
                                    if lat:
                                        return fin
                                    fin()

                        jobs.append(dict(qreads=[("qk", 2 * c + m, bidx)], q_ap=qk[rows, 2 * c + m, tl:tl + Nq], keys=kl,
                                         scale=scale, Nq=Nq, epi=epi))
        run_jobs(jobs, hook=hook, hook_every=hook_every, keep_tail=lat)

    def attn_C(P, l, hook=None):
        lat = P["lat"]
        jobs = []
        for c in range(2):
            hs_ = [{}, {}]
            for qi, (tl, Nq, bidx, seq) in enumerate(q_blocks(P)):
                for hh in range(2):
                    h = 2 * c + hh
                    rows = slice(64 * hh, 64 * hh + 64)

                    def pre(hsd=hs_[hh], h=h):
                        if (l, h) in strip_pref:
                            hsd["s"] = strip_pref.pop((l, h))
                            return
                        sp_, spk = strip_rot.get()
                        dma("pool", sp_[:], strips_d[l, h], [], [spk], spk)
                        hsd["s"] = (sp_, spk)

                    kl = []
                    if lat:
                        for j in range(4):
                            kl.append(dict(kT=cKC[rows, c, 128 * j:128 * j + 128], kreads=["cKC%d" % c],
                                           v=cVC[:, j, offA[h]:offA[h] + 128], vreads=[("cVC", x) for x in range(5)]))
                        qb = bidx
                        for kc in (range(0, 6) if qb == 0 else range(2, 8)):
                            e0 = 10 - 2 * kc + 8 * qb
                            kl.append(dict(kT=qk[rows, 8 + c, 128 * kc:128 * kc + 128], kreads=[("qk", 8 + c, kc // 4)],
                                           v=VC[:, kc, offA[h]:offA[h] + 128], vreads=[("VC", kc)], lazy=(hs_[hh], e0, kc, qb)))
                    else:
                        for (ko, tile) in new_keys(P, seq):
                            kl.append(dict(kT=qk[rows, 8 + c, ko:ko + 128], kreads=[("qk", 8 + c, ko // 512)],
                                           v=VC[:, tile, offA[h]:offA[h] + 128], vreads=[("VC", tile)]))

                    def epi(pso, pko, c=c, hh=hh, tl=tl, Nq=Nq, bidx=bidx):
                        rc, rck, o_lo = norm_out(pso, pko, Nq, hh == 1, on_act=True)
                        tt(brg[o_lo:o_lo + 64, 4 + c, tl:tl + Nq], rc[o_lo:o_lo + 64, 0:Nq], brg[o_lo:o_lo + 64, 4 + c, tl:tl + Nq], ALU.mult,
                           [rck, ("brg", 4 + c, bidx)], [("brg", 4 + c, bidx)])

                    jobs.append(dict(qreads=[("qk", 6 + c, bidx)], q_ap=qk[rows, 6 + c, tl:tl + Nq], keys=kl, scale=0.125, Nq=Nq, epi=epi,
                                     pre=(pre if (lat and qi == 0) else None)))
        run_jobs_lazy(jobs, hook)

    def run_jobs_lazy(jobs, hook=None):
        for job in jobs:
            opre = job.get("pre")

            def pre2(job=job, opre=opre):
                if opre is not None:
                    opre()
                for kd in job["keys"]:
                    if "lazy" in kd:
                        hs_, e0, kc, qb = kd["lazy"]
                        sp_, spk = hs_["s"]
                        kd["extra"] = [(i8_b, sp_[:, e0 * 64:e0 * 64 + 512], [spk, "cstb"]),
                                       (cstb[0:16, C_MA + 128 * kc:C_MA + 128 * kc + 128],
                                        cstb[0:16, C_MB + 512 * qb:C_MB + 512 * qb + 512], ["cstb"])]
            job["pre"] = pre2
        run_jobs(jobs, hook=hook, hook_every=1)

    def attn_D(P, l, hook=None):
        lat = P["lat"]
        jobs = []
        for kv in range(2):
            for (tl, Nq, bidx, seq) in q_blocks(P):
                for g in range(2):
                    qh = 2 * kv + g
                    rows = slice(64 * g, 64 * g + 64)
                    kchunk = 12 if kv == g else 13
                    ccol = 0 if kv == g else 1
                    vsl = (64 + 128 * kv) if g == 0 else (128 * kv)
                    kl = []
                    if lat:
                        for j in range(4):
                            kl.append(dict(kT=cKD[rows, ccol, 128 * j:128 * j + 128], kreads=["cKD%d" % ccol],
                                           v=cVD[:, j, vsl:vsl + 128], vreads=[("cVD", x) for x in range(3)]))
                        qb = bidx
                        for kc in (range(0, 5) if qb == 0 else range(3, 8)):
                            x0 = 512 * qb - 128 * kc + 512
                            ex = [(ident_b, cstb[:, C_BAND + x0:C_BAND + x0 + 512], ["cstb"])]
                            kl.append(dict(kT=qk[rows, kchunk, 128 * kc:128 * kc + 128], kreads=[("qk", kchunk, kc // 4)],
                                           v=VD[:, kc, vsl:vsl + 128], vreads=[("VD", kc)], extra=ex))
                    else:
                        for (ko, tile) in new_keys(P, seq):
                            kl.append(dict(kT=qk[rows, kchunk, ko:ko + 128], kreads=[("qk", kchunk, ko // 512)],
                                           v=VD[:, tile, vsl:vsl + 128], vreads=[("VD", tile)]))

                    def epi(pso, pko, kv=kv, g=g, qh=qh, tl=tl, Nq=Nq, bidx=bidx):
                        rc, rck, o_lo = norm_out(pso, pko, Nq, g == 1, add_es=12 + qh, on_act=True)
                        tt(brg[o_lo:o_lo + 64, 6 + kv, tl:tl + Nq], rc[o_lo:o_lo + 64, 0:Nq], brg[o_lo:o_lo + 64, 6 + kv, tl:tl + Nq], ALU.mult,
                           [rck, ("brg", 6 + kv, bidx)], [("brg", 6 + kv, bidx)])

                    jobs.append(dict(qreads=[("qk", 10 + kv, bidx)], q_ap=qk[rows, 10 + kv, tl:tl + Nq], keys=kl, scale=0.125, Nq=Nq, epi=epi))
        run_jobs(jobs, hook=hook, hook_every=1)

    strip_rot = Rot("strip", stripb)

    def merge_and_out(P, l, between=None):
        nb = P["T"] // 512
        sm = small[:, l, :]
        g = P["g"]
        for oc in range(8):
            wt, wk = load_w(lambda n, oc=oc: w_mg[l][:, n, oc * 128:(oc + 1) * 128].rearrange("(kc p) m -> p kc m", p=128), four=True,
                            cache=(l, 7 + oc, P["lat"]))
            wo_, wok = wbo_rot.get()
            if P["lat"]:
                dma("pool", wo_[:], w_bo[l][:, :, oc * 128:(oc + 1) * 128].rearrange("n (kc p) m -> p (n kc) m", p=128), [], [wok], wok)
                dma("sp", wsc_bo[l, oc], wo_[:], [wok], [("wscbo", l, oc)], ("wsto", wok[1]))
            else:
                dma("pool", wo_[:], wsc_bo[l, oc], [("wscbo", l, oc)], [wok], wok)
            for b in range(nb):
                macc, mk = Fp.get()
                for n in range(4):
                    psg, pkg = PSp.get()
                    for kc in range(8):
                        mm(psg[:], wt[:, kc, n * 128:(n + 1) * 128], hsl(kc, 512 * b, 512), kc == 0, kc == 7, wk + [("hT", kc * 2 + b)], [pkg])
                    psp, pkp = PSp.get()
                    for k2 in range(2):
                        mm(psp[:], wo_[:, n * 2 + k2, :], brg[:, 2 * n + k2, 512 * b:512 * b + 512], k2 == 0, k2 == 1,
                           [wok, ("brg", 2 * n + k2, b)], [pkp])
                    gt, gk = Fp.get()
                    act(gt[:], psg[:], AF.Sigmoid, [pkg, "small"], [gk], bias=sm[:, 64 + n * 8 + oc:65 + n * 8 + oc])
                    if n == 0:
                        tt(macc[:], psp[:], gt[:], ALU.mult, [pkp, gk], [mk])
                    else:
                        tt(gt[:], psp[:], gt[:], ALU.mult, [pkp, gk], [gk])
                        if n < 3:
                            tt(macc[:], macc[:], gt[:], ALU.add, [mk, gk], [mk])
                        else:
                            tt(qk[:, oc, 512 * b:512 * b + 512], macc[:], gt[:], ALU.add, [mk, gk], [("qk", oc, b)])
        if between is not None:
            between()
        for j in range(2):
            wt, wk = load_w(w_o[l][:, j * 512:(j + 1) * 512].rearrange("(kc p) n -> p kc n", p=128), cache=(l, 15 + j, P["lat"]))
            for o4 in range(4):
                oc = 4 * j + o4
                for b in range(nb):
                    tok0 = P["t0"] + 512 * b
                    ps, pk = PSp.get()
                    for kc in range(8):
                        mm(ps[:], wt[:, kc, o4 * 128:(o4 + 1) * 128], qk[:, kc, 512 * b:512 * b + 512], kc == 0, kc == 7, wk + [("qk", kc, b)], [pk])
                    stt(xT[:, oc, tok0:tok0 + 512], ps[:], modTs[l][:, 16 + oc, g:g + 1], xT[:, oc, tok0:tok0 + 512], ALU.mult, ALU.add,
                        [pk, ("modT", l), ("xT", oc, tok0 // 512)], [("xT", oc, tok0 // 512)])

    passes = [dict(name="L", t0=0, T=LT, g=0, lat=True), dict(name="P", t0=LT, T=PT, g=1, lat=False)]
    stage = [0]

    def go():
        stage[0] += 1
        return STOP is None or stage[0] <= STOP

    def load_x(mg):
        for tt_i in range(NT // 128):
            blk3 = tt_i // 4
            if tt_i % 2 == 1:
                next(mg, None)
            for half in range(2):
                stg, sk = Fp.get()
                dma("sp", stg[:], x_all[tt_i * 128:(tt_i + 1) * 128, half * 512:(half + 1) * 512], [], [sk], ("Fd", sk[1]))
                ps, pk = PSp.get()
                for j in range(4):
                    S.op("pe", lambda e, ps=ps, stg=stg, j=j: e.transpose(ps[:, j * 128:(j + 1) * 128], stg[:, j * 128:(j + 1) * 128], ident_f),
                         reads=[sk, "cstf"], writes=[pk])
                dst = xT[:, half * 4:half * 4 + 4, tt_i * 128:(tt_i + 1) * 128]
                src = ps[:].rearrange("p (a b) -> p a b", a=4)
                keys = [("xT", half * 4 + j, blk3) for j in range(4)]
                if (tt_i + half) % 2 == 0:
                    act(dst, src, AF.Copy, [pk], keys)
                else:
                    cp(dst, src, [pk], keys)


    modg = {}
    h_done = {}
    cst2_loaded = []
    for l in range(DEPTH):
        if not go():
            break
        if l == 0:
            mg0 = mod_gen(0, PSp, blocks=(0, 1, 2, 3))
            load_x(mg0)
            for _ in mg0:
                pass
        layer_setup(l)
        for P in passes:
            nb = P["T"] // 512
            if not go():
                break
            if not h_done.get((l, P["name"])):
                for b in range(nb):
                    emit_h(P, b, l)
            if not go():
                break
            projections(P, l)
            if not go():
                break
            if not go():
                break
            if P["lat"]:
                cg = load_cache(l)
                scr = [(qk[:, k, :].bitcast(F32), [("qk", k, 0), ("qk", k, 1)]) for k in range(6)]
                lg = lru(P, l, scr=scr, pool=PSa, fine=True)
                for _ in range(4):
                    next(lg, None)
                for h_ in range(2):
                    sp_, spk = strip_rot.get()
                    dma("pool", sp_[:], strips_d[l, h_], [], [spk], spk)
                    strip_pref[(l, h_)] = (sp_, spk)
                gens = ([mod_gen(0, PSa, blocks=(4, 5))] if l == 0 else []) + ([mod_gen(l + 1, PSa)] if l + 1 < DEPTH else [])

                def chain(gs):
                    for g_ in gs:
                        for _ in g_:
                            yield

                mgn = chain(gens)
                hc = [0]

                def ahook(cg=cg, mgn=mgn, hc=hc):
                    hc[0] += 1
                    if hc[0] % 4 == 2:
                        next(mgn, None)
                    else:
                        next(cg, None)

                attn_A(P, l, hook=ahook, hook_every=1)
                for _ in cg:
                    pass
                for _ in mgn:
                    pass
            else:
                attn_A(P, l)
            if not go():
                break
            if P["lat"]:
                lhook = lambda: next(lg, None)
            else:
                lg, lhook = None, None
            attn_C(P, l, hook=lhook)
            if not go():
                break
            attn_D(P, l, hook=lhook)
            if lg is not None:
                for _ in lg:
                    pass
            if not go():
                break
            if P["lat"]:
                fence("post")
            if P["lat"]:
                for b in range(nb):
                    emit_h(P, b, l, lnexp=True)
            if P["lat"]:
                nxt = (passes[1], l)
            else:
                nxt = (passes[0], l + 1) if l + 1 < DEPTH else None

            def between(nxt=nxt):
                if nxt is None or STOP is not None:
                    return
                Pn, ln = nxt
                for b in range(Pn["T"] // 512):
                    emit_h(Pn, b, ln)
                h_done[(ln, Pn["name"])] = True

            merge_and_out(P, l, between=between)
        if STOP is not None and stage[0] > STOP:
            break

    if STOP is not None:
        dbg_brg = dout("dbg_brg", [128, 8, 1024])
        dbg_qk = dout("dbg_qk", [128, 14, 1024])
        dbg_xT = dout("dbg_xT", [128, 8, NT])
        dbg_xc = dout("dbg_xc", [128, 2, 1024])
        dma("pool", dbg_brg, brg[:], [("brg", c, b) for c in range(8) for b in range(2)], [], "dbg0")
        dma("pool", dbg_qk, qk[:], [("qk", c, b) for c in range(14) for b in range(2)], [], "dbg1")
        dma("sp", dbg_xT, xT[:], [("xT", c, b) for c in range(8) for b in range(3)], [], "dbg2")
        dma("pool", dbg_xc, xc[:], [("xc", c, b) for c in range(2) for b in range(2)], [], "dbg3")
    for blk3 in range(NT // 512):
        tok0 = blk3 * 512
        r, rk = rms_rstd(tok0, blk3)
        for kc in range(8):
            yt, yk = Fp.get()
            stt(yt[:], xT[:, kc, tok0:tok0 + 512], normf[:, kc:kc + 1], r[:], ALU.mult, ALU.mult, [("xT", kc, blk3), "normf", rk], [yk])
            ps, pk = PSp.get()
            for t4 in range(4):
                S.op("pe", lambda e, ps=ps, yt=yt, t4=t4: e.transpose(ps[:, t4 * 128:(t4 + 1) * 128], yt[:, t4 * 128:(t4 + 1) * 128], ident_f),
                     reads=[yk, "cstf"], writes=[pk])
            og, ok = Fp.get()
            if kc % 2 == 0:
                act(og[:], ps[:], AF.Copy, [pk], [ok])
            else:
                cp(og[:], ps[:], [pk], [ok])
            dma("sp", y_all[tok0:tok0 + 512, kc * 128:(kc + 1) * 128].rearrange("(t p) f -> p t f", p=128),
                og[:].rearrange("p (t f) -> p t f", t=4), [ok], [], ("Fd", ok[1]))

    with nc.allow_non_contiguous_dma(reason="small strided cache/state layouts"):
        S.emit(nc, st)
    return nc, st, S


def _host_constants():
    cb = np.zeros((128, NCB), np.float32)
    cf = np.zeros((128, NCF), np.float32)
    eye = np.eye(128, dtype=np.float32)
    cb[:, C_ID:C_ID + 128] = eye
    cb[:, C_I8:C_I8 + 128] = 8.0 * eye
    cb[:, C_ONES:C_ONES + 128] = 1.0
    p = np.arange(128)
    cb[:, C_BONES:C_BONES + 128] = (p[:, None] // 64 == p[None, :] // 64).astype(np.float32)
    t = np.arange(LT)
    row, col = t // 64, t % 64
    RA = np.zeros((128, 128), np.float32)
    RD = np.zeros((128, 128), np.float32)
    for pp in range(128):
        j = pp % 32
        jj = j % 16
        i = jj % 8
        inv = 10000.0 ** (-(i / 8.0))
        pos = row if j < 16 else col
        ang = pos.astype(np.float32) * np.float32(inv)
        cb[pp, C_COSA:C_COSA + LT] = np.cos(ang)
        if jj < 8:
            cb[pp, C_SINA:C_SINA + LT] = -np.sin(ang)
            RA[pp + 8, pp] = 1.0
        else:
            cb[pp, C_SINA:C_SINA + LT] = np.sin(ang)
            RA[pp - 8, pp] = 1.0
        j = pp % 64
        jj = j % 32
        i = jj % 16
        inv = 10000.0 ** (-(i / 16.0))
        pos = row if j < 32 else col
        ang = pos.astype(np.float32) * np.float32(inv)
        cb[pp, C_COSD:C_COSD + LT] = np.cos(ang)
        if jj < 16:
            cb[pp, C_SIND:C_SIND + LT] = -np.sin(ang)
            RD[pp + 16, pp] = 1.0
        else:
            cb[pp, C_SIND:C_SIND + LT] = np.sin(ang)
            RD[pp - 16, pp] = 1.0
    cb[:, C_RA:C_RA + 128] = RA
    cb[:, C_RD:C_RD + 128] = RD
    x = np.arange(1152)
    cb[:, C_BAND:C_BAND + 1152] = np.where(np.abs(x[None, :] - 512 - p[:, None]) <= 128, 0.0, MASKV)
    rows = 16
    row_start = np.clip(np.arange(rows) - 4, 0, rows - 8)
    krow = np.arange(LT) // 64
    for r in range(16):
        ok = (krow >= row_start[r]) & (krow < row_start[r] + 8)
        cb[r, C_MA:C_MA + LT] = np.where(ok, 0.0, MASKV)
        cb[r, C_MB:C_MB + LT] = (krow == r).astype(np.float32)
    cf[:, 0:128] = eye
    cf[:, 128] = ((p % 64) < 32).astype(np.float32)
    cf[:, 129] = ((p % 64) >= 32).astype(np.float32)
    cf[:, 130] = EPS
    return cb, cf


def _strips(na_rpb):
    out = np.zeros((DEPTH, 4, 128, 22, 64), np.float32)
    qcol = np.arange(64)
    kcol = np.arange(64)
    col_start = np.clip(qcol - 8, 0, 48)
    ok = (kcol[:, None] >= col_start[None, :]) & (kcol[:, None] < col_start[None, :] + 16)
    dcol = np.clip(kcol[:, None] - qcol[None, :] + 15, 0, 30)
    for krl in range(2):
        for e in range(22):
            d = 17 - e + krl
            if 0 <= d <= 14:
                blk = na_rpb[:, :, d, :][:, :, dcol]
                blk = np.where(ok[None, None], blk, np.float32(-1000.0))
                out[:, :, krl * 64:(krl + 1) * 64, e, :] = blk
    return out.reshape(DEPTH, 4, 128, NSTRIP)


_CACHE = {}


def kernel(x_prompt, x_sample, cache_diff_k, cache_diff_v, cache_na_k, cache_na_v, cache_swa_k, cache_swa_v,
           state_lru, c, c_ctx, norm_g, w_ada, b_ada, w_in, diff_lambda, diff_norm_g, conv_w, conv_b,
           lru_wa, lru_ba, lru_wx, lru_bx, lru_lam, na_rpb, swa_sink, w_mg, b_mg, w_bo, w_o, norm_f):
    f = lambda a: np.ascontiguousarray(np.asarray(a, dtype=np.float32))
    (x_prompt, x_sample, cache_diff_k, cache_diff_v, cache_na_k, cache_na_v, cache_swa_k, cache_swa_v, state_lru, c, c_ctx,
     norm_g, w_ada, b_ada, w_in, diff_lambda, diff_norm_g, conv_w, conv_b, lru_wa, lru_ba, lru_wx, lru_bx, lru_lam, na_rpb,
     swa_sink, w_mg, b_mg, w_bo, w_o, norm_f) = map(f, (
        x_prompt, x_sample, cache_diff_k, cache_diff_v, cache_na_k, cache_na_v, cache_swa_k, cache_swa_v, state_lru, c, c_ctx,
        norm_g, w_ada, b_ada, w_in, diff_lambda, diff_norm_g, conv_w, conv_b, lru_wa, lru_ba, lru_wx, lru_bx, lru_lam, na_rpb,
        swa_sink, w_mg, b_mg, w_bo, w_o, norm_f))
    if "nc" not in _CACHE:
        _CACHE["nc"] = build_program()
    nc, _st, S = _CACHE["nc"]
    cb, cf = _host_constants()
    strips = _strips(na_rpb)

    def fm(v, nch):
        return v.reshape(nch, 128).T

    lruw = np.zeros((DEPTH, 128, 8, 128), np.float32)
    for l in range(DEPTH):
        for d in range(2):
            for kind, w in enumerate((lru_wa, lru_wx)):
                for cc in range(2):
                    for hb in range(2):
                        lruw[l, hb * 64:(hb + 1) * 64, d * 4 + kind * 2 + cc, hb * 64:(hb + 1) * 64] = w[l, d, 2 * cc + hb]
    normf = fm(norm_f, 8)
    in_maps = []
    for core in range(NCORES):
        small = np.zeros((128, DEPTH, NS), np.float32)
        for l in range(DEPTH):
            s = small[:, l, :]
            s[:, 0:16] = np.repeat(fm(norm_g[l], 8), 2, axis=1)
            s[:, 16:64] = np.repeat(fm(b_ada[l], 24), 2, axis=1)
            for n in range(4):
                s[:, 64 + n * 8:72 + n * 8] = fm(b_mg[l, n], 8)
            for cc in range(2):
                for j in range(4):
                    s[:, 96 + cc * 4 + j] = conv_w[l, j, cc * 128:(cc + 1) * 128]
                s[:, 104 + cc] = conv_b[l, cc * 128:(cc + 1) * 128]
                for d in range(2):
                    s[:, 106 + d * 2 + cc] = lru_ba[l, d, cc * 128:(cc + 1) * 128]
                    s[:, 110 + d * 2 + cc] = lru_bx[l, d, cc * 128:(cc + 1) * 128]
                    s[:, 114 + d * 2 + cc] = lru_lam[l, d, cc * 128:(cc + 1) * 128]
                    s[:, 123 + d * 2 + cc] = state_lru[core, l, d, cc * 128:(cc + 1) * 128]
            s[:, 118] = np.tile(diff_norm_g[l], 2)
            s[:, 119:123] = swa_sink[l][None, :]
            s[:, 128:256] = diff_lambda[l].reshape(1, 128)
        cT = np.stack([fm(c[core], 8), fm(c_ctx, 8)], axis=2)
        x_all = np.concatenate([x_sample[core], x_prompt[2 * core], x_prompt[2 * core + 1]], axis=0)
        in_maps.append(dict(
            x_all=np.ascontiguousarray(x_all), w_ada=w_ada, w_in=w_in, w_mg=w_mg, w_bo=w_bo, w_o=w_o,
            small=small, cT=np.ascontiguousarray(cT), normf=np.ascontiguousarray(normf), lruw=lruw, strips=strips,
            cstb=cb, cstf=cf, cdk=cache_diff_k[core], cdv=cache_diff_v[core], cnk=cache_na_k[core], cnv=cache_na_v[core],
            csk=cache_swa_k[core], csv=cache_swa_v[core]))
    res = run_bass_kernel_spmd(nc, in_maps, core_ids=list(range(NCORES)))
    R = res.results
    y_sample = np.stack([R[i]["y_all"][:LT] for i in range(NCORES)], axis=0)
    y_prompt = np.concatenate([R[i]["y_all"][LT:].reshape(2, 256, D) for i in range(NCORES)], axis=0)
    cat = lambda k: np.concatenate([R[i][k] for i in range(NCORES)], axis=0)
    if STOP is not None:
        _CACHE["dbg"] = dict(brg=R[0]["dbg_brg"], qk=R[0]["dbg_qk"], xT=R[0]["dbg_xT"], xc=R[0]["dbg_xc"])
    return (y_prompt, y_sample, cat("ndk"), cat("ndv"), cat("nnk"), cat("nnv"), cat("nsk"), cat("nsv"), cat("nst"))
```
